# Optimizing a Trainium2 kernel written in Bass

```python
import jax, jax.numpy as jnp
from jax import lax
import numpy as np

D_MODEL = 4096
BATCH = 4
SEQ = 2048
DEPTH = 1

D_MIX = D_MODEL
D_CONV = D_MIX // 2
D_SGU = D_MIX - D_CONV
CONV_GROUPS = 16
SGU_HEADS = 16
SGU_HEAD_DIM = D_SGU // SGU_HEADS
CHUNK = 128
CONV_WIDTH = 31
D_FF = 11008
FFN_CONV_WIDTH = 3
N_MOD = 6
EPS = 1e-6

kernel_name = "hybrid_conformer_conv_sgu_adaln_block"


def rms_norm(x, g):
    xf = x.astype(jnp.float32)
    y = xf * lax.rsqrt(jnp.mean(xf * xf, axis=-1, keepdims=True) + EPS)
    return (y * g.astype(jnp.float32)).astype(x.dtype)


def layer_norm(x, g, b):
    xf = x.astype(jnp.float32)
    mu = jnp.mean(xf, axis=-1, keepdims=True)
    xc = xf - mu
    var = jnp.mean(xc * xc, axis=-1, keepdims=True)
    y = xc * lax.rsqrt(var + EPS) * g.astype(jnp.float32) + b.astype(jnp.float32)
    return y.astype(x.dtype)


def depthwise_conv_centred(x, w, b):
    k = w.shape[0]
    pad = (k - 1) // 2
    y = lax.conv_general_dilated(
        x, w[:, None, :].astype(x.dtype), window_strides=(1,), padding=[(pad, pad)],
        dimension_numbers=("NWC", "WIO", "NWC"), feature_group_count=x.shape[-1])
    return y + b


def modulate(x, g, shift, scale):
    return rms_norm(x, g) * (1 + scale[:, None, :]) + shift[:, None, :]


def setup_inputs(seed: int = 0) -> dict:
    key = jax.random.key(seed)
    ks = jax.random.split(key, 24)
    L = DEPTH

    def nrm(k, shape, std):
        return std * jax.random.normal(k, shape, jnp.float32)

    return {
        "x": nrm(ks[0], (BATCH, SEQ, D_MODEL), 1.0),
        "c": nrm(ks[1], (BATCH, D_MODEL), 1.0),
        "w_ada": nrm(ks[2], (L, D_MODEL, N_MOD * D_MODEL), 0.5 * D_MODEL ** -0.5),
        "b_ada": nrm(ks[3], (L, N_MOD * D_MODEL), 0.02),
        "g_mix": 1.0 + nrm(ks[4], (L, D_MODEL), 0.02),
        "w_in": nrm(ks[5], (L, D_MODEL, 2 * D_MIX), D_MODEL ** -0.5),
        "conv_w": nrm(ks[6], (L, CONV_WIDTH, D_CONV), CONV_WIDTH ** -0.5),
        "conv_b": nrm(ks[7], (L, D_CONV), 0.02),
        "conv_ln_g": 1.0 + nrm(ks[8], (L, D_CONV), 0.02),
        "conv_ln_b": nrm(ks[9], (L, D_CONV), 0.02),
        "sgu_ln_g": 1.0 + nrm(ks[10], (L, D_SGU), 0.02),
        "sgu_ln_b": nrm(ks[11], (L, D_SGU), 0.02),
        "sgu_w": nrm(ks[12], (L, SGU_HEADS, CHUNK, CHUNK), CHUNK ** -0.5),
        "sgu_b": 1.0 + nrm(ks[13], (L, SGU_HEADS, CHUNK), 0.02),
        "out_g_conv": 1.0 + nrm(ks[14], (L, D_CONV), 0.02),
        "out_g_sgu": 1.0 + nrm(ks[15], (L, D_SGU), 0.02),
        "w_out": nrm(ks[16], (L, D_MIX, D_MODEL), D_MIX ** -0.5),
        "g_ffn": 1.0 + nrm(ks[17], (L, D_MODEL), 0.02),
        "w_up": nrm(ks[18], (L, D_MODEL, 2 * D_FF), D_MODEL ** -0.5),
        "ffn_conv_w": nrm(ks[19], (L, FFN_CONV_WIDTH, D_FF), FFN_CONV_WIDTH ** -0.5),
        "ffn_conv_b": nrm(ks[20], (L, D_FF), 0.02),
        "w_down": nrm(ks[21], (L, D_FF, D_MODEL), D_FF ** -0.5),
        "g_final": 1.0 + nrm(ks[22], (D_MODEL,), 0.02),
    }


def reference(x, c, w_ada, b_ada, g_mix, w_in, conv_w, conv_b, conv_ln_g, conv_ln_b,
              sgu_ln_g, sgu_ln_b, sgu_w, sgu_b, out_g_conv, out_g_sgu, w_out,
              g_ffn, w_up, ffn_conv_w, ffn_conv_b, w_down, g_final):
    bsz, seq, _ = x.shape
    n_chunks = seq // CHUNK
    c_act = jax.nn.silu(c)

    for l in range(DEPTH):
        mod = (c_act @ w_ada[l] + b_ada[l]).reshape(bsz, N_MOD, D_MODEL)
        shift_m, scale_m, gate_m = mod[:, 0], mod[:, 1], mod[:, 2]
        shift_f, scale_f, gate_f = mod[:, 3], mod[:, 4], mod[:, 5]

        h = modulate(x, g_mix[l], shift_m, scale_m)
        proj = h @ w_in[l]
        p_conv = proj[..., :2 * D_CONV]
        p_sgu = proj[..., 2 * D_CONV:]

        a = p_conv[..., :D_CONV] * jax.nn.sigmoid(p_conv[..., D_CONV:])
        a = depthwise_conv_centred(a, conv_w[l], conv_b[l])
        a = jax.nn.silu(layer_norm(a, conv_ln_g[l], conv_ln_b[l]))

        z = jax.nn.gelu(p_sgu, approximate=False)
        u, v = z[..., :D_SGU], z[..., D_SGU:]
        v = layer_norm(v, sgu_ln_g[l], sgu_ln_b[l])
        v = v.reshape(bsz, n_chunks, CHUNK, SGU_HEADS, SGU_HEAD_DIM)
        v = jnp.einsum("hpq,bcqhd->bcphd", sgu_w[l], v) + sgu_b[l].T[:, :, None]
        bgrp = u * v.reshape(bsz, seq, D_SGU)

        y = jnp.concatenate([rms_norm(a, out_g_conv[l]), rms_norm(bgrp, out_g_sgu[l])], axis=-1)
        x = x + gate_m[:, None, :] * (y @ w_out[l])

        h = modulate(x, g_ffn[l], shift_f, scale_f)
        up = h @ w_up[l]
        gte = depthwise_conv_centred(up[..., :D_FF], ffn_conv_w[l], ffn_conv_b[l])
        act = jax.nn.silu(gte) * up[..., D_FF:]
        x = x + gate_f[:, None, :] * (act @ w_down[l])

    return rms_norm(x, g_final)
```

```python
import numpy as np
from contextlib import ExitStack
import concourse.bass as bass
import concourse.mybir as mybir
from concourse.bass_utils import run_bass_kernel_spmd

F32 = mybir.dt.float32
BF16 = mybir.dt.bfloat16
AF = mybir.ActivationFunctionType
ALU = mybir.AluOpType
AX = mybir.AxisListType

D = 4096
DH = 2048
DFF = 11008
NJ = 86
EPS = 1e-6
NSLOT = 3
SAME_SYNC = True
NTILES = 2


class Trk:
    def __init__(self, nc, es):
        self.nc = nc
        self.es = es
        self.eng = {"pe": nc.tensor, "act": nc.scalar, "dve": nc.vector, "sp": nc.sync, "pool": nc.gpsimd}
        self.semh = {}
        self.cnt = {}
        for k in ("pe", "act", "dve"):
            self.semh["s_" + k] = es.enter_context(nc.semaphore("s_" + k))
            self.cnt["s_" + k] = 0
        self.seen = {k: {} for k in self.eng}
        self.lastw = {}
        self.readers = {}
        self.softw = {}
        self._soft_now = set()

    def _collect(self, e, reads, writes):
        own = "s_" + e
        deps = {}

        def add(n, v):
            if deps.get(n, 0) < v:
                deps[n] = v

        for r in reads:
            ev = self.lastw.get(r)
            if ev:
                add(*ev)
        for w in writes:
            ev = self.lastw.get(w)
            if ev and not (ev[0] == own and w in self._soft_now and self.softw.get(w)):
                add(*ev)
            for n, v in self.readers.get(w, {}).items():
                if n == own:
                    continue
                add(n, v)
        if own in deps and (e == "pe" or not SAME_SYNC):
            del deps[own]
        return deps

    def _wait(self, e, deps):
        for n, v in deps.items():
            if self.seen[e].get(n, 0) >= v:
                continue
            self.eng[e].wait_ge(self.semh[n], v)
            self.seen[e][n] = v

    def _record(self, ev, reads, writes):
        n, v = ev
        for r in reads:
            d = self.readers.setdefault(r, {})
            if d.get(n, 0) < v:
                d[n] = v
        for w in writes:
            if w in self._soft_now and self.softw.get(w) and self.lastw.get(w, ("", 0))[0] != n:
                pass
            self.softw[w] = w in self._soft_now
            self.lastw[w] = ev
            self.readers[w] = {}

    def _norm(self, reads, writes):
        isps = lambda k: isinstance(k, tuple) and k[0] == "ps"
        r2 = [k for k in reads if not isps(k)]
        soft = [k for k in reads if isps(k) and k not in writes]
        self._soft_now = set(soft)
        w2 = list(writes) + soft
        return r2, w2

    def op(self, e, fn, reads=(), writes=()):
        reads, writes = self._norm(reads, writes)
        self._wait(e, self._collect(e, reads, writes))
        ins = fn(self.eng[e])
        own = "s_" + e
        self.cnt[own] += 1
        ins.then_inc(self.semh[own], 1)
        self._record((own, self.cnt[own]), reads, writes)

    def group(self, e, fns, reads=(), writes=()):
        reads, writes = self._norm(reads, writes)
        self._wait(e, self._collect(e, reads, writes))
        ins = None
        for fn in fns:
            ins = fn(self.eng[e])
        own = "s_" + e
        self.cnt[own] += 1
        ins.then_inc(self.semh[own], 1)
        self._record((own, self.cnt[own]), reads, writes)

    def dma(self, q, skey, outs_ins, reads=(), writes=()):
        if skey not in self.semh:
            self.semh[skey] = self.es.enter_context(self.nc.semaphore(skey))
            self.cnt[skey] = 0
        self._soft_now = set()
        self._wait(q, self._collect(q, reads, writes))
        for (o, i) in outs_ins:
            self.eng[q].dma_start(out=o, in_=i).then_inc(self.semh[skey], 16)
            self.cnt[skey] += 16
        self._record((skey, self.cnt[skey]), reads, writes)

    def barrier(self, engines=("act", "dve", "sp")):
        for e in engines:
            deps = {}
            for n, c in self.cnt.items():
                if c > 0 and not n.startswith("W"):
                    if n == "s_pe" and e == "pe":
                        continue
                    deps[n] = c
            self._wait(e, deps)

    def final_wait(self, e):
        deps = {n: c for n, c in self.cnt.items() if c > 0}
        self._wait(e, deps)


def build_program():
    nc = bass.Bass("TRN2", target_bir_lowering=False)

    def din(name, shape):
        return nc.dram_tensor(name, list(shape), F32, kind="ExternalInput").ap()

    x_ext = din("x_ext", [1152, D])
    c_l_d = din("c_l", [128, 32])
    w_ada = din("w_ada", [D, 6 * D])
    bada_bc = din("bada_bc", [128, 6 * D])
    w_in = din("w_in", [D, 2 * D])
    w_out = din("w_out", [D, D])
    w_up = din("w_up", [D, 2 * DFF])
    w_down = din("w_down", [DFF, D])
    ident_d = din("ident", [128, 128])
    gmix_d = din("gmix", [128, 32])
    gffn_d = din("gffn", [128, 32])
    convw_d = din("convw", [128, 16 * 31])
    convb_d = din("convb", [128, 16])
    clng_d = din("clng", [128, 16])
    clnb_d = din("clnb", [128, 16])
    ogc_d = din("ogc", [128, 16])
    ogs_d = din("ogs", [128, 16])
    fcw_d = din("fcw", [128, NJ * 3])
    fcb_d = din("fcb", [128, NJ])
    wts_d = din("wts", [128, DH])
    slng_d = din("slng_bc", [128, DH])
    slnb_d = din("slnb_bc", [128, DH])
    sb_d = din("sb_bc", [128, DH])
    gfin_d = din("gfin_bc", [128, D])
    out_d = nc.dram_tensor("out", [1024, D], F32, kind="ExternalOutput").ap()
    gsc_d = nc.dram_tensor("gsc", [2, 128, D], F32, kind="ExternalOutput").ap()

    with ExitStack() as es:
        tk = Trk(nc, es)

        uniq = {"n": 0}

        def sb(name, shape, dt=F32, stack=es):
            uniq["n"] += 1
            return stack.enter_context(nc.sbuf_tensor("sb%d_%s" % (uniq["n"], name), list(shape), dt))

        Wt = [sb("W%d" % i, [128, 8192], BF16) for i in range(NSLOT)]
        ps = [es.enter_context(nc.psum_tensor("ps%d" % i, [128, 512], F32)) for i in range(8)]
        ident = sb("ident", [128, 128])
        ones_bf = sb("ones_bf", [128, 128], BF16)
        one_f = sb("one_f", [128, 2])
        c_l = sb("c_lt", [128, 32])
        c_act = sb("c_act", [128, 32])
        gmix = sb("gmixt", [128, 32])
        gffn = sb("gffnt", [128, 32])
        vecs = sb("vecs", [128, 4, 32])
        gs_m = sb("gs_m", [128, 32])
        gs_f = sb("gs_f", [128, 32])
        convw = sb("convwt", [128, 16, 31])
        convb = sb("convbt", [128, 16])
        clng = sb("clngt", [128, 16])
        clnb = sb("clnbt", [128, 16])
        ogc = sb("ogct", [128, 16])
        ogs = sb("ogst", [128, 16])
        fcw = sb("fcwt", [128, NJ, 3])
        fcb = sb("fcbt", [128, NJ])
        WTs = sb("WTs", [128, 16, 128], BF16)
        asave = sb("asave", [128, 16, 15])
        gsave = sb("gsave", [128, NJ])
        small = sb("small", [128, 64])
        rinv = sb("rinv", [128, 16])
        actT = sb("actT", [128, 32, 513], BF16)

        wstate = {"n": 0, "last": {}}
        nW = {"n": NSLOT}

        pend = set()

        def add_slots(stack, k):
            for _ in range(k):
                Wt.append(sb("Wx%d" % nW["n"], [128, 8192], BF16, stack))
                nW["n"] += 1
                wstate["last"][len(Wt) - 1] = wstate["n"]
                pend.add(len(Wt) - 1)

        def drop_slots(k):
            for _ in range(k):
                Wt.pop()

        def wload(src_list):
            s = min(range(len(Wt)), key=lambda i: wstate["last"].get(i, -1))
            wstate["n"] += 1
            wstate["last"][s] = wstate["n"]
            if s in pend:
                tk.barrier(engines=("pool",))
                pend.clear()
            oi = []
            for (dst, src) in src_list:
                oi.append((dst(Wt[s]), src))
            tk.dma("pool", "Wsem%d" % s, oi, reads=(), writes=[("W", s)])
            return s

        def wK(s):
            return Wt[s][:].rearrange("p (k n) -> p k n", k=32)

        def w_k_cols(W_d, c0):
            return [(lambda t: t[:].rearrange("p (k n) -> p k n", k=32),
                     W_d[:, c0:c0 + 256].rearrange("(k p) n -> p k n", p=128))]

        bstate = {"n": 0}

        def newbank():
            b = bstate["n"] % 6
            bstate["n"] += 1
            return b

        plist = [(ident, ident_d), (c_l, c_l_d), (gmix, gmix_d), (gffn, gffn_d), (convb, convb_d),
                 (clng, clng_d), (clnb, clnb_d), (ogc, ogc_d), (ogs, ogs_d), (fcb, fcb_d)]
        oi = [(t[:], d[:, :]) for t, d in plist]
        oi.append((convw[:].rearrange("p a b -> p (a b)"), convw_d[:, :]))
        oi.append((fcw[:].rearrange("p a b -> p (a b)"), fcw_d[:, :]))
        pkeys = ["ident", "c_l", "gmix", "gffn", "convb", "clng", "clnb", "ogc", "ogs", "fcb", "convw", "fcw"]
        tk.dma("sp", "par", oi, reads=(), writes=pkeys)
        tk.dma("pool", "wts", [(WTs[:].rearrange("p a b -> p (a b)"), wts_d[:, :])], reads=(), writes=["WTs"])
        tk.op("dve", lambda e: e.memset(ones_bf[:], 1.0), writes=["ones_bf"])
        tk.op("dve", lambda e: e.memset(one_f[:], 1.0), writes=["one_f"])

        def rstd_from(var_ap, out_ap, tmp_ap, n, rkeys, wkey, scale=1.0):
            tk.op("dve", lambda e: e.tensor_scalar(out=tmp_ap, in0=var_ap, scalar1=scale, scalar2=EPS,
                                                   op0=ALU.mult, op1=ALU.add), reads=rkeys, writes=[("tmp", wkey)])
            tk.op("act", lambda e: e.activation(out=tmp_ap, in_=tmp_ap, func=AF.Sqrt), reads=[("tmp", wkey)],
                  writes=[("tmp", wkey)])
            tk.op("dve", lambda e: e.reciprocal(out=out_ap, in_=tmp_ap), reads=[("tmp", wkey)], writes=[wkey])

        ADA_DEFER = (10, 11, 22, 23)

        def ada_load(n):
            return [wload([(lambda t: t[:].rearrange("p (k n) -> p k n", k=8),
                            w_ada[kq * 1024:(kq + 1) * 1024, n * 1024:(n + 1) * 1024].rearrange("(k p) n -> p k n", p=128))])
                    for kq in range(4)]

        def ada_compute(n, slots, bufs):
            cT, badat, atmp, gstg = bufs["cT"], bufs["badat"], bufs["atmp"], bufs["gstg"]
            which, r = divmod(n, 4)
            W8 = [Wt[sl][:].rearrange("p (k n) -> p k n", k=8) for sl in slots]
            for half in range(2):
                c0 = n * 1024 + half * 512
                r2 = 2 * r + half
                bi = (2 * n + half) % 2
                tk.dma("sp", "bada%d" % bi, [(badat[bi][:], bada_bc[:, c0:c0 + 512])], reads=(),
                       writes=[("bada", bi)])
                pb = newbank()
                tk.group("pe", [(lambda e, k=k: e.matmul(ps[pb][:, 0:512], lhsT=cT[:, k, :],
                                                        rhs=W8[k // 8][:, k % 8, half * 512:(half + 1) * 512],
                                                        start=(k == 0), stop=(k == 31))) for k in range(32)],
                         reads=[("W", sl) for sl in slots] + ["cT"], writes=[("ps", pb)])
                if which in (2, 5):
                    gi = 0 if which == 2 else 1
                    tk.op("dve", lambda e: e.tensor_tensor(out=gstg[bi][:], in0=ps[pb][:, 0:512], in1=badat[bi][:],
                                                           op=ALU.add),
                          reads=[("ps", pb), ("bada", bi)], writes=[("gstg", bi)])
                    tk.dma("sp", "gst%d" % bi, [(gsc_d[gi, :, r2 * 512:(r2 + 1) * 512], gstg[bi][:])],
                           reads=[("gstg", bi)], writes=[("gsc", gi, r2)])
                else:
                    vi = {0: 0, 1: 1, 3: 2, 4: 3}[which]
                    tk.op("dve", lambda e: e.tensor_tensor(out=atmp[bi][:], in0=ps[pb][:, 0:512], in1=badat[bi][:],
                                                           op=ALU.add),
                          reads=[("ps", pb), ("bada", bi)], writes=[("atmp", bi)])
                    a3 = atmp[bi][:].rearrange("p (a b) -> p a b", a=4)
                    tk.op("dve", lambda e: e.tensor_tensor(out=a3, in0=a3,
                                                           in1=ident[:].unsqueeze(1).broadcast_to([128, 4, 128]),
                                                           op=ALU.mult),
                          reads=[("atmp", bi), "ident"], writes=[("atmp", bi)])
                    tk.op("dve", lambda e: e.tensor_reduce(out=vecs[:, vi, 4 * r2:4 * r2 + 4], in_=a3, axis=AX.X, op=ALU.add),
                          reads=[("atmp", bi)], writes=[("vecs", vi, r2)])

        def ada_scope_bufs(stack):
            cTx = sb("cTx", [128, 32, 128], BF16, stack)
            bufs = dict(cT=cTx,
                        badat=[sb("badatx%d" % i, [128, 512], F32, stack) for i in range(2)],
                        atmp=None,
                        gstg=[sb("gstgx%d" % i, [128, 512], F32, stack) for i in range(2)])
            tk.op("dve", lambda e: e.tensor_copy(out=cTx[:], in_=c_act[:].unsqueeze(2).broadcast_to([128, 32, 128])),
                  reads=["c_act"], writes=["cT"])
            return bufs

        with ExitStack() as st:
            cT = sb("cT", [128, 32, 128], BF16, st)
            add_slots(st, 5)
            badat = [sb("badat%d" % i, [128, 512], F32, st) for i in range(2)]
            atmp = [sb("atmp%d" % i, [128, 512], F32, st) for i in range(2)]
            gstg = [sb("gstg%d" % i, [128, 512], F32, st) for i in range(2)]
            tk.op("act", lambda e: e.activation(out=c_act[:], in_=c_l[:], func=AF.Silu), reads=["c_l"], writes=["c_act"])
            tk.op("dve", lambda e: e.tensor_copy(out=cT[:], in_=c_act[:].unsqueeze(2).broadcast_to([128, 32, 128])),
                  reads=["c_act"], writes=["cT"])
            ada_bufs = dict(cT=cT, badat=badat, atmp=atmp, gstg=gstg)
            for n in range(24):
                if n in ADA_DEFER:
                    continue
                ada_compute(n, ada_load(n), ada_bufs)
            for (vi, gsrc, gdst, gk, dk) in ((1, gmix, gs_m, "gmix", "gs_m"), (3, gffn, gs_f, "gffn", "gs_f")):
                tk.op("dve", lambda e: e.tensor_scalar(out=gdst[:], in0=vecs[:, vi, :], scalar1=1.0, scalar2=None,
                                                       op0=ALU.add),
                      reads=[("vecs", vi, r) for r in range(8)], writes=[dk])
                tk.op("dve", lambda e: e.tensor_tensor(out=gdst[:], in0=gdst[:], in1=gsrc[:], op=ALU.mult),
                      reads=[dk, gk], writes=[dk])
            tk.barrier()
            drop_slots(5)
        sh_m = vecs[:, 0, :]
        sh_f = vecs[:, 2, :]
        VK = lambda vi: [("vecs", vi, r) for r in range(8)]

        def token_stats_rstd(src_ap_fn, npart, st8, mv, dst_ap, rkeys, key):
            for c8 in range(8):
                tk.op("dve", lambda e: e.bn_stats(out=st8[0:npart, c8, :], in_=src_ap_fn(c8)), reads=rkeys,
                      writes=[(key, "st", c8)])
            tk.op("dve", lambda e: e.bn_aggr(out=mv[0:npart, 0:2], in_=st8[0:npart].rearrange("p a b -> p (a b)")),
                  reads=[(key, "st", c8) for c8 in range(8)], writes=[(key, "mv")])
            tk.op("dve", lambda e: e.tensor_tensor(out=mv[0:npart, 2:3], in0=mv[0:npart, 0:1], in1=mv[0:npart, 0:1],
                                                   op=ALU.mult), reads=[(key, "mv")], writes=[(key, "m2")])
            tk.op("dve", lambda e: e.tensor_tensor(out=mv[0:npart, 3:4], in0=mv[0:npart, 2:3], in1=mv[0:npart, 1:2],
                                                   op=ALU.add), reads=[(key, "m2"), (key, "mv")], writes=[(key, "ms")])
            rstd_from(mv[0:npart, 3:4], dst_ap, mv[0:npart, 4:5], npart, [(key, "ms")], (key, "rstd"))

        def to_feature_major(src_blk_fn, blk, gs, shv, gkeys, rkeys, dst_key):
            for g in range(8):
                pb = newbank()
                tk.group("pe", [(lambda e, i=i: e.transpose(out=ps[pb][:, i * 128:(i + 1) * 128],
                                                           in_=src_blk_fn((4 * g + i)), identity=ident[:]))
                                for i in range(4)], reads=rkeys + ["ident"], writes=[("ps", pb)])
                for i in range(4):
                    j = 4 * g + i
                    yield_dst = dst_key(j)
                    if g % 2 == 0:
                        tk.op("act", lambda e: e.activation(out=yield_dst[0], in_=ps[pb][:, i * 128:(i + 1) * 128],
                                                            func=AF.Identity, scale=gs[:, j:j + 1], bias=shv[:, j:j + 1]),
                              reads=[("ps", pb)] + gkeys, writes=[yield_dst[1]])
                    else:
                        tk.op("dve", lambda e: e.tensor_scalar(out=yield_dst[0], in0=ps[pb][:, i * 128:(i + 1) * 128],
                                                               scalar1=gs[:, j:j + 1], scalar2=shv[:, j:j + 1],
                                                               op0=ALU.mult, op1=ALU.add),
                              reads=[("ps", pb)] + gkeys, writes=[yield_dst[1]])

        for tau in range(NTILES):
            tb0 = 512 * tau
            yTa = actT[:, 0:16, :]
            yTb = actT[:, 16:32, :]
            with ExitStack() as mx:
                hT = sb("hT", [128, 32, 640], BF16, mx)
                st8 = sb("st8", [128, 8, 6], F32, mx)
                mv = sb("mv", [128, 8], F32, mx)
                with ExitStack() as sx:
                    xt = [sb("xt%d" % i, [128, D], F32, sx) for i in range(2)]
                    def x_prep(blk):
                        xs = blk % 2
                        tk.dma("sp", "xt%d" % xs, [(xt[xs][:], x_ext[tb0 + 128 * blk: tb0 + 128 * blk + 128, :])],
                               reads=(), writes=[("xt", xs)] + [("xtn", xs, c8) for c8 in range(8)])
                        token_stats_rstd(lambda c8: xt[xs][:, c8 * 512:(c8 + 1) * 512], 128, st8, mv, rsx[:, blk:blk + 1],
                                         [("xt", xs)], ("sx", blk))
                        for c8 in range(8):
                            tk.op("act", lambda e: e.activation(out=xt[xs][:, c8 * 512:(c8 + 1) * 512],
                                                                in_=xt[xs][:, c8 * 512:(c8 + 1) * 512], func=AF.Identity,
                                                                scale=rsx[:, blk:blk + 1]),
                                  reads=[("xt", xs), (("sx", blk), "rstd")], writes=[("xtn", xs, c8)])

                    def x_tr(blk):
                        xs = blk % 2
                        to_feature_major(lambda j: xt[xs][:, j * 128:(j + 1) * 128], blk, gs_m, sh_m,
                                         ["gs_m"] + VK(0), [("xt", xs)] + [("xtn", xs, c8) for c8 in range(8)],
                                         lambda j: (hT[:, j, blk * 128:(blk + 1) * 128], ("hT", blk, j)))

                    rsx = sb("rsx", [128, 8], F32, sx)
                    if tau == 0:
                        abufs = ada_scope_bufs(sx)
                        add_slots(sx, 1)
                        sl_a = ada_load(10)
                    x_prep(0)
                    for blk in range(5):
                        if blk + 1 < 5:
                            x_prep(blk + 1)
                        x_tr(blk)
                        if tau == 0 and blk == 1:
                            ada_compute(10, sl_a, abufs)
                            sl_a = ada_load(11)
                        if tau == 0 and blk == 4:
                            ada_compute(11, sl_a, abufs)
                    tk.barrier()
                    if tau == 0:
                        drop_slots(1)
                HT = lambda blks: [("hT", b, j) for b in blks for j in range(32)]

                with ExitStack() as sa:
                    conv_out = sb("conv_out", [128, 16, 513], F32, sa)
                    a_pad = [sb("a_pad%d" % i, [128, 543], BF16, sa) for i in range(2)]
                    Dg = [sb("Dg%d" % i, [128, 31, 128], BF16, sa) for i in range(2)]
                    hcv = sb("hcv", [128, 32], F32, sa)
                    sig = [sb("sig%d" % i, [128, 528], F32, sa) for i in range(2)]
                    cbq = [sb("cbq%d" % i, [128, 2, 513], BF16, sa) for i in range(2)]
                    hs = sb("hs", [128, 32], BF16, sa)
                    m1 = sb("m1", [128, 513], F32, sa)
                    m2 = sb("m2", [128, 513], F32, sa)
                    rs = sb("rs", [128, 513], F32, sa)
                    hred = sb("hred", [128, 2], F32, sa)
                    for i in range(2):
                        tk.op("dve", lambda e: e.memset(a_pad[i][:, 0:15], 0.0), writes=[("a_pad", i)])
                    pend_conv = []
                    for ip in range(8):
                        sL = wload(w_k_cols(w_in, ip * 256))
                        sG = wload(w_k_cols(w_in, 2048 + ip * 256))
                        WL = wK(sL)
                        WG = wK(sG)
                        hb = (ip % 2) * 64
                        bL = []
                        for c2 in range(2):
                            pb = newbank()
                            bL.append(pb)
                            tk.group("pe", [(lambda e, k=k: e.matmul(ps[pb][:, 0:512], lhsT=WL[:, k, c2 * 128:(c2 + 1) * 128],
                                                                    rhs=hT[:, k, 0:512], start=(k == 0), stop=(k == 31)))
                                            for k in range(32)],
                                     reads=[("W", sL)] + HT([0, 1, 2, 3]), writes=[("ps", pb)])
                            hc = hb + c2 * 16
                            tk.group("pe", [(lambda e, k=k: e.matmul(ps[7][:, hc:hc + 16], lhsT=WL[:, k, c2 * 128:(c2 + 1) * 128],
                                                                    rhs=hT[:, k, 512:528], start=(k == 0), stop=(k == 31)))
                                            for k in range(32)],
                                     reads=[("W", sL)] + HT([4]), writes=[("ps", 7)])
                        for c2 in range(2):
                            cc = 2 * ip + c2
                            ab = cc % 2
                            pb = newbank()
                            tk.group("pe", [(lambda e, k=k: e.matmul(ps[pb][:, 0:512], lhsT=WG[:, k, c2 * 128:(c2 + 1) * 128],
                                                                    rhs=hT[:, k, 0:512], start=(k == 0), stop=(k == 31)))
                                            for k in range(32)],
                                     reads=[("W", sG)] + HT([0, 1, 2, 3]), writes=[("ps", pb)])
                            hg = hb + 32 + c2 * 16
                            hl = hb + c2 * 16
                            tk.group("pe", [(lambda e, k=k: e.matmul(ps[7][:, hg:hg + 16], lhsT=WG[:, k, c2 * 128:(c2 + 1) * 128],
                                                                    rhs=hT[:, k, 512:528], start=(k == 0), stop=(k == 31)))
                                            for k in range(32)],
                                     reads=[("W", sG)] + HT([4]), writes=[("ps", 7)])
                            for f in pend_conv:
                                f()
                            pend_conv.clear()
                            tk.op("act", lambda e: e.activation(out=sig[ab][:, 0:512], in_=ps[pb][:, 0:512], func=AF.Sigmoid),
                                  reads=[("ps", pb)], writes=[("sig", ab, 0)])
                            tk.op("act", lambda e: e.activation(out=sig[ab][:, 512:528], in_=ps[7][:, hg:hg + 16], func=AF.Sigmoid),
                                  reads=[("ps", 7)], writes=[("sig", ab, 1)])
                            tk.op("dve", lambda e: e.tensor_tensor(out=a_pad[ab][:, 15:527], in0=ps[bL[c2]][:, 0:512],
                                                                   in1=sig[ab][:, 0:512], op=ALU.mult),
                                  reads=[("ps", bL[c2]), ("sig", ab, 0)], writes=[("a_pad", ab)])
                            tk.op("dve", lambda e: e.tensor_tensor(out=a_pad[ab][:, 527:543], in0=ps[7][:, hl:hl + 16],
                                                                   in1=sig[ab][:, 512:528], op=ALU.mult),
                                  reads=[("ps", 7), ("sig", ab, 1)], writes=[("a_pad", ab)])
                            if tau == 0:
                                tk.op("act", lambda e: e.activation(out=asave[:, cc, :], in_=a_pad[ab][:, 512:527], func=AF.Copy),
                                      reads=[("a_pad", ab)], writes=[("asave", cc)])
                            else:
                                tk.op("dve", lambda e: e.tensor_copy(out=a_pad[ab][:, 0:15], in_=asave[:, cc, :]),
                                      reads=[("asave", cc)], writes=[("a_pad", ab)])
                            def conv_emit(cc=cc, ab=ab):
                                tk.op("dve", lambda e: e.tensor_tensor(out=Dg[ab][:], in0=ident[:].unsqueeze(1).broadcast_to([128, 31, 128]),
                                                                       in1=convw[:, cc, :].unsqueeze(2).broadcast_to([128, 31, 128]),
                                                                       op=ALU.mult),
                                      reads=["ident", "convw"], writes=[("Dg", ab)])
                                pc = newbank()
                                tk.group("pe", [(lambda e, k=k: e.matmul(ps[pc][:, 0:512], lhsT=Dg[ab][:, k, :], rhs=a_pad[ab][:, k:k + 512],
                                                                        start=(k == 0), stop=(k == 30))) for k in range(31)],
                                         reads=[("Dg", ab), ("a_pad", ab)], writes=[("ps", pc)])
                                tk.op("act", lambda e: e.activation(out=conv_out[:, cc, 0:512], in_=ps[pc][:, 0:512], func=AF.Identity,
                                                                    bias=convb[:, cc:cc + 1], scale=1.0),
                                      reads=[("ps", pc), "convb"], writes=[("co", cc)])
                                tk.op("dve", lambda e: e.tensor_tensor(out=hcv[:, 0:31], in0=a_pad[ab][:, 512:543], in1=convw[:, cc, :],
                                                                       op=ALU.mult), reads=[("a_pad", ab), "convw"], writes=["hcv"])
                                tk.op("dve", lambda e: e.tensor_reduce(out=hcv[:, 31:32], in_=hcv[:, 0:31], axis=AX.X, op=ALU.add),
                                      reads=["hcv"], writes=["hcv1"])
                                tk.op("dve", lambda e: e.tensor_tensor(out=conv_out[:, cc, 512:513], in0=hcv[:, 31:32],
                                                                       in1=convb[:, cc:cc + 1], op=ALU.add),
                                      reads=["hcv1", "convb"], writes=[("co", cc, "h")])

                            pend_conv.append(conv_emit)
                    for f in pend_conv:
                        f()
                    pend_conv.clear()
                    bS1 = newbank()
                    bS2 = newbank()
                    for cc in range(16):
                        q = cc % 2
                        tk.op("act", lambda e: e.activation(out=cbq[q][:, 0, 0:512], in_=conv_out[:, cc, 0:512], func=AF.Copy),
                              reads=[("co", cc)], writes=[("cbq", q, 0)])
                        tk.op("act", lambda e: e.activation(out=cbq[q][:, 1, 0:512], in_=conv_out[:, cc, 0:512], func=AF.Square),
                              reads=[("co", cc)], writes=[("cbq", q, 1)])
                        tk.group("pe", [lambda e: e.matmul(ps[bS1][:, 0:512], lhsT=ones_bf[:], rhs=cbq[q][:, 0, 0:512],
                                                           start=(cc == 0), stop=(cc == 15))],
                                 reads=[("cbq", q, 0), "ones_bf"], writes=[("ps", bS1)])
                        tk.group("pe", [lambda e: e.matmul(ps[bS2][:, 0:512], lhsT=ones_bf[:], rhs=cbq[q][:, 1, 0:512],
                                                           start=(cc == 0), stop=(cc == 15))],
                                 reads=[("cbq", q, 1), "ones_bf"], writes=[("ps", bS2)])
                    COK = [("co", cc) for cc in range(16)] + [("co", cc, "h") for cc in range(16)]
                    tk.op("act", lambda e: e.activation(out=hs[:, 0:16], in_=conv_out[:, :, 512], func=AF.Copy),
                          reads=COK, writes=["hs0"])
                    tk.op("act", lambda e: e.activation(out=hs[:, 16:32], in_=conv_out[:, :, 512], func=AF.Square),
                          reads=COK, writes=["hs1"])
                    bH = newbank()
                    tk.group("pe", [lambda e: e.matmul(ps[bH][:, 0:32], lhsT=ones_bf[:], rhs=hs[:], start=True, stop=True)],
                             reads=["hs0", "hs1", "ones_bf"], writes=[("ps", bH)])
                    tk.op("dve", lambda e: e.tensor_reduce(out=hred[:], in_=ps[bH][:, 0:32].rearrange("p (a b) -> p a b", a=2),
                                                           axis=AX.X, op=ALU.add), reads=[("ps", bH)], writes=["hred"])
                    tk.op("dve", lambda e: e.tensor_scalar(out=m1[:, 0:512], in0=ps[bS1][:, 0:512], scalar1=1.0 / DH, scalar2=None,
                                                           op0=ALU.mult), reads=[("ps", bS1)], writes=["m1a"])
                    tk.op("dve", lambda e: e.tensor_scalar(out=m2[:, 0:512], in0=ps[bS2][:, 0:512], scalar1=1.0 / DH, scalar2=None,
                                                           op0=ALU.mult), reads=[("ps", bS2)], writes=["m2a"])
                    tk.op("dve", lambda e: e.tensor_scalar(out=m1[:, 512:513], in0=hred[:, 0:1], scalar1=1.0 / DH, scalar2=None,
                                                           op0=ALU.mult), reads=["hred"], writes=["m1b"])
                    tk.op("dve", lambda e: e.tensor_scalar(out=m2[:, 512:513], in0=hred[:, 1:2], scalar1=1.0 / DH, scalar2=None,
                                                           op0=ALU.mult), reads=["hred"], writes=["m2b"])
                    tk.op("dve", lambda e: e.tensor_tensor(out=rs[:], in0=m1[:], in1=m1[:], op=ALU.mult),
                          reads=["m1a", "m1b"], writes=["rs"])
                    tk.op("dve", lambda e: e.tensor_tensor(out=m2[:], in0=m2[:], in1=rs[:], op=ALU.subtract),
                          reads=["m2a", "m2b", "rs"], writes=["m2a", "m2b"])
                    rstd_from(m2[:], rs[:], m2[:], 128, ["m2a", "m2b"], "rsA")
                    for cc in range(16):
                        q = cc % 2
                        acc = conv_out[:, cc, :]
                        tk.op("dve", lambda e: e.tensor_tensor(out=acc, in0=acc, in1=m1[:], op=ALU.subtract),
                              reads=[("co", cc), ("co", cc, "h"), "m1a", "m1b"], writes=[("co", cc), ("co", cc, "h")])
                        tk.op("dve", lambda e: e.tensor_tensor(out=acc, in0=acc, in1=rs[:], op=ALU.mult),
                              reads=[("co", cc), "rsA"], writes=[("co", cc)])
                        tk.op("act", lambda e: e.activation(out=acc, in_=acc, func=AF.Silu, scale=clng[:, cc:cc + 1],
                                                            bias=clnb[:, cc:cc + 1]),
                              reads=[("co", cc), "clng", "clnb"], writes=[("co", cc)])
                        tk.op("dve", lambda e: e.tensor_scalar(out=yTa[:, cc, :], in0=acc, scalar1=ogc[:, cc:cc + 1], scalar2=None,
                                                               op0=ALU.mult), reads=[("co", cc), "ogc"], writes=[("yT", cc)])
                        tk.op("act", lambda e: e.activation(out=cbq[q][:, 0, :], in_=acc, func=AF.Square),
                              reads=[("co", cc)], writes=[("cbq", q, 0)])
                        fns = [(lambda e, b=b: e.matmul(ps[6][:, b * 16 + cc:b * 16 + cc + 1], lhsT=cbq[q][:, 0, b * 128:(b + 1) * 128],
                                                        rhs=ones_bf[:, 0:1], start=True, stop=True)) for b in range(4)]
                        fns.append(lambda e: e.matmul(ps[6][0:1, 64 + cc:65 + cc], lhsT=cbq[q][:, 0, 512:513], rhs=ones_bf[:, 0:1],
                                                      start=True, stop=True))
                        tk.group("pe", fns, reads=[("cbq", q, 0), "ones_bf"], writes=[("ps", 6)])
                    tk.op("dve", lambda e: e.tensor_reduce(out=small[:, 0:4], in_=ps[6][:, 0:64].rearrange("p (a b) -> p a b", a=4),
                                                           axis=AX.X, op=ALU.add), reads=[("ps", 6)], writes=["ssa"])
                    tk.op("dve", lambda e: e.tensor_reduce(out=small[0:1, 4:5], in_=ps[6][0:1, 64:80], axis=AX.X, op=ALU.add),
                          reads=[("ps", 6)], writes=["ssah"])
                    rstd_from(small[:, 0:4], rinv[:, 0:4], small[:, 8:12], 128, ["ssa"], "rinva", scale=1.0 / DH)
                    rstd_from(small[0:1, 4:5], rinv[0:1, 4:5], small[0:1, 12:13], 1, ["ssah"], "rinvah", scale=1.0 / DH)
                    tk.barrier()

                vln = sb("vln", [128, 5, DH], BF16, mx)
                with ExitStack() as sv:
                    zv = sb("zv", [128, 5, DH], F32, sv)
                    slnx = sb("slnx", [128, DH], F32, sv)
                    stv = sb("stv", [128, 5, 8, 6], F32, sv)
                    mvv = sb("mvv", [128, 5, 4], F32, sv)
                    tk.dma("sp", "sln", [(slnx[:], slng_d[:, :])], reads=(), writes=["slnx"])
                    for n in range(8):
                        s = wload(w_k_cols(w_in, 6144 + n * 256))
                        W3 = wK(s)
                        for blk in range(5):
                            pb = newbank()
                            tk.group("pe", [(lambda e, k=k: e.matmul(ps[pb][:, 0:256], lhsT=hT[:, k, blk * 128:(blk + 1) * 128],
                                                                    rhs=W3[:, k, :], start=(k == 0), stop=(k == 31)))
                                            for k in range(32)],
                                     reads=[("W", s)] + HT([blk]), writes=[("ps", pb)])
                            tk.op("act", lambda e: e.activation(out=zv[:, blk, n * 256:(n + 1) * 256], in_=ps[pb][:, 0:256],
                                                                func=AF.Gelu), reads=[("ps", pb)], writes=[("zv", blk, n)])
                            tk.op("dve", lambda e: e.bn_stats(out=stv[:, blk, n, :], in_=zv[:, blk, n * 256:(n + 1) * 256]),
                                  reads=[("zv", blk, n)], writes=[("stv", blk, n)])
                    for blk in range(5):
                        ZK = [("zv", blk, n) for n in range(8)]
                        tk.op("dve", lambda e: e.bn_aggr(out=mvv[:, blk, 0:2], in_=stv[:, blk].rearrange("p a b -> p (a b)")),
                              reads=[("stv", blk, n) for n in range(8)], writes=[("mvv", blk)])
                        rstd_from(mvv[:, blk, 1:2], mvv[:, blk, 2:3], mvv[:, blk, 3:4], 128, [("mvv", blk)], ("mvv", blk, "r"))
                        tk.op("dve", lambda e: e.tensor_scalar(out=zv[:, blk, :], in0=zv[:, blk, :], scalar1=mvv[:, blk, 0:1],
                                                               scalar2=mvv[:, blk, 2:3], op0=ALU.subtract, op1=ALU.mult),
                              reads=ZK + [("mvv", blk), ("mvv", blk, "r")], writes=ZK)
                        tk.op("dve", lambda e: e.tensor_tensor(out=zv[:, blk, :], in0=zv[:, blk, :], in1=slnx[:], op=ALU.mult),
                              reads=ZK + ["slnx"], writes=ZK)
                    tk.dma("sp", "sln", [(slnx[:], slnb_d[:, :])], reads=(), writes=["slnx"])
                    for blk in range(5):
                        ZK = [("zv", blk, n) for n in range(8)]
                        tk.op("dve", lambda e: e.tensor_tensor(out=vln[:, blk, :], in0=zv[:, blk, :], in1=slnx[:], op=ALU.add),
                              reads=ZK + ["slnx"], writes=[("vln", blk)])
                    tk.barrier()

                with ExitStack() as su:
                    sbb = sb("sbb", [128, DH], F32, su)
                    ub = [sb("ub%d" % i, [128, 513], F32, su) for i in range(2)]
                    tb = [sb("tb%d" % i, [128, 513], F32, su) for i in range(2)]
                    sq = [sb("sq%d" % i, [128, 513], BF16, su) for i in range(2)]
                    tk.dma("sp", "sbb", [(sbb[:], sb_d[:, :])], reads=(), writes=["sbb"])
                    pend_stat = []
                    for ip in range(8):
                        s = wload(w_k_cols(w_in, 4096 + ip * 256))
                        W3 = wK(s)
                        for c2 in range(2):
                            h = 2 * ip + c2
                            q = h % 2
                            pb = newbank()
                            tk.group("pe", [(lambda e, k=k: e.matmul(ps[pb][:, 0:512], lhsT=W3[:, k, c2 * 128:(c2 + 1) * 128],
                                                                    rhs=hT[:, k, 0:512], start=(k == 0), stop=(k == 31)))
                                            for k in range(32)],
                                     reads=[("W", s)] + HT([0, 1, 2, 3]), writes=[("ps", pb)])
                            hc = q * 8
                            tk.group("pe", [(lambda e, k=k: e.matmul(ps[7][:, hc:hc + 1], lhsT=W3[:, k, c2 * 128:(c2 + 1) * 128],
                                                                    rhs=hT[:, k, 512:513], start=(k == 0), stop=(k == 31)))
                                            for k in range(32)],
                                     reads=[("W", s)] + HT([4]), writes=[("ps", 7)])
                            for f in pend_stat:
                                f()
                            pend_stat.clear()
                            tk.op("act", lambda e: e.activation(out=ub[q][:, 0:512], in_=ps[pb][:, 0:512], func=AF.Gelu),
                                  reads=[("ps", pb)], writes=[("ub", q, 0)])
                            tk.op("act", lambda e: e.activation(out=ub[q][:, 512:513], in_=ps[7][:, hc:hc + 1], func=AF.Gelu),
                                  reads=[("ps", 7)], writes=[("ub", q, 1)])
                            pm = newbank()
                            fns = [(lambda e, c=c: e.matmul(ps[pm][:, c * 128:(c + 1) * 128], lhsT=vln[:, c, h * 128:(h + 1) * 128],
                                                            rhs=WTs[:, h, :], start=True, stop=True)) for c in range(4)]
                            tk.group("pe", fns, reads=[("vln", c) for c in range(4)] + ["WTs"], writes=[("ps", pm)])
                            tk.group("pe", [lambda e: e.matmul(ps[7][:, hc + 1:hc + 2], lhsT=vln[:, 4, h * 128:(h + 1) * 128],
                                                               rhs=WTs[:, h, 0:1], start=True, stop=True)],
                                     reads=[("vln", 4), "WTs"], writes=[("ps", 7)])
                            tk.op("dve", lambda e: e.tensor_tensor(
                                out=tb[q][:, 0:512].rearrange("p (a b) -> p a b", a=4),
                                in0=ps[pm][:, 0:512].rearrange("p (a b) -> p a b", a=4),
                                in1=sbb[:, h * 128:(h + 1) * 128].unsqueeze(1).broadcast_to([128, 4, 128]), op=ALU.add),
                                  reads=[("ps", pm), "sbb"], writes=[("tb", q, 0)])
                            tk.op("dve", lambda e: e.tensor_tensor(out=tb[q][:, 512:513], in0=ps[7][:, hc + 1:hc + 2],
                                                                   in1=sbb[:, h * 128:h * 128 + 1], op=ALU.add),
                                  reads=[("ps", 7), "sbb"], writes=[("tb", q, 1)])
                            tk.op("dve", lambda e: e.tensor_tensor(out=tb[q][:], in0=tb[q][:], in1=ub[q][:], op=ALU.mult),
                                  reads=[("tb", q, 0), ("tb", q, 1), ("ub", q, 0), ("ub", q, 1)], writes=[("tb", q, 0), ("tb", q, 1)])
                            tk.op("dve", lambda e: e.tensor_scalar(out=yTb[:, h, :], in0=tb[q][:], scalar1=ogs[:, h:h + 1], scalar2=None,
                                                                   op0=ALU.mult), reads=[("tb", q, 0), ("tb", q, 1), "ogs"],
                                  writes=[("yT", 16 + h)])
                            tk.op("act", lambda e: e.activation(out=sq[q][:], in_=tb[q][:], func=AF.Square),
                                  reads=[("tb", q, 0), ("tb", q, 1)], writes=[("sq", q)])
                            def stat_emit(h=h, q=q):
                                fns = [(lambda e, b=b: e.matmul(ps[6][:, b * 16 + h:b * 16 + h + 1], lhsT=sq[q][:, b * 128:(b + 1) * 128],
                                                                rhs=ones_bf[:, 0:1], start=True, stop=True)) for b in range(4)]
                                fns.append(lambda e: e.matmul(ps[6][0:1, 64 + h:65 + h], lhsT=sq[q][:, 512:513], rhs=ones_bf[:, 0:1],
                                                              start=True, stop=True))
                                tk.group("pe", fns, reads=[("sq", q), "ones_bf"], writes=[("ps", 6)])

                            pend_stat.append(stat_emit)
                    for f in pend_stat:
                        f()
                    pend_stat.clear()
                    tk.op("dve", lambda e: e.tensor_reduce(out=small[:, 16:20], in_=ps[6][:, 0:64].rearrange("p (a b) -> p a b", a=4),
                                                           axis=AX.X, op=ALU.add), reads=[("ps", 6)], writes=["ssb"])
                    tk.op("dve", lambda e: e.tensor_reduce(out=small[0:1, 20:21], in_=ps[6][0:1, 64:80], axis=AX.X, op=ALU.add),
                          reads=[("ps", 6)], writes=["ssbh"])
                    rstd_from(small[:, 16:20], rinv[:, 8:12], small[:, 24:28], 128, ["ssb"], "rinvb", scale=1.0 / DH)
                    rstd_from(small[0:1, 20:21], rinv[0:1, 12:13], small[0:1, 28:29], 1, ["ssbh"], "rinvbh", scale=1.0 / DH)
                    tk.barrier()

            with ExitStack() as fx:
                big = sb("big", [128, 4, D], F32, fx)
                st8 = sb("st8b", [128, 8, 6], F32, fx)
                mv = sb("mvb", [128, 8], F32, fx)
                sxm = ExitStack()
                xmh = sb("xmh", [1, D], F32, sxm)
                YT = [("yT", j) for j in range(32)]
                with ExitStack() as so:
                    gmb = sb("gmb", [128, D], F32, so)
                    t1 = [sb("t1_%d" % i, [128, 256], F32, so) for i in range(2)]
                    t2 = [sb("t2_%d" % i, [128, 256], F32, so) for i in range(2)]
                    tk.dma("sp", "gmb", [(gmb[:], gsc_d[0, :, :])], reads=[("gsc", 0, r) for r in range(8)], writes=["gmb"])
                    for blk in range(4):
                        tk.dma("sp", "xm%d" % blk, [(big[:, blk, :], x_ext[tb0 + 128 * blk: tb0 + 128 * blk + 128, :])],
                               reads=(), writes=[("xm", blk, n) for n in range(16)])
                    tk.dma("sp", "xmh", [(xmh[0:1, :], x_ext[tb0 + 512: tb0 + 513, :])], reads=(),
                           writes=[("xmh", n) for n in range(16)])
                    cntr = 0
                    for n in range(16):
                        s = wload(w_k_cols(w_out, n * 256))
                        W3 = wK(s)
                        for blk in range(5):
                            pb = newbank()
                            q = cntr % 2
                            cntr += 1
                            if blk < 4:
                                ts = slice(blk * 128, (blk + 1) * 128)
                                P = slice(0, 128)
                                ra = rinv[:, blk:blk + 1]
                                rb = rinv[:, 8 + blk:9 + blk]
                                dst = big[:, blk, n * 256:(n + 1) * 256]
                                dkey = ("xm", blk, n)
                            else:
                                ts = slice(512, 513)
                                P = slice(0, 1)
                                ra = rinv[0:1, 4:5]
                                rb = rinv[0:1, 12:13]
                                dst = xmh[0:1, n * 256:(n + 1) * 256]
                                dkey = ("xmh", n)
                            fns = [(lambda e, k=k: e.matmul(ps[pb][P, 0:256], lhsT=actT[:, k, ts], rhs=W3[:, k, :],
                                                            start=(k == 0), stop=(k == 15))) for k in range(16)]
                            fns += [(lambda e, k=k: e.matmul(ps[pb][P, 256:512], lhsT=actT[:, k, ts], rhs=W3[:, k, :],
                                                             start=(k == 16), stop=(k == 31))) for k in range(16, 32)]
                            tk.group("pe", fns, reads=[("W", s)] + YT, writes=[("ps", pb)])
                            rk = ["rinva", "rinvah", "rinvb", "rinvbh"]
                            tk.op("act", lambda e: e.activation(out=t1[q][P, :], in_=ps[pb][P, 0:256], func=AF.Identity, scale=ra),
                                  reads=[("ps", pb)] + rk, writes=[("t1", q)])
                            tk.op("dve", lambda e: e.scalar_tensor_tensor(out=t2[q][P, :], in0=ps[pb][P, 256:512], scalar=rb,
                                                                          in1=t1[q][P, :], op0=ALU.mult, op1=ALU.add),
                                  reads=[("ps", pb), ("t1", q)] + rk, writes=[("t2", q)])
                            tk.op("dve", lambda e: e.tensor_tensor(out=t2[q][P, :], in0=t2[q][P, :],
                                                                   in1=gmb[P, n * 256:(n + 1) * 256], op=ALU.mult),
                                  reads=[("t2", q), "gmb"], writes=[("t2", q)])
                            tk.op("dve", lambda e: e.tensor_tensor(out=dst, in0=dst, in1=t2[q][P, :], op=ALU.add),
                                  reads=[("t2", q), dkey], writes=[dkey])
                    tk.barrier()
                rsn = sb("rsn", [128, 8], F32, sxm)

                def n_prep(blk):
                    XK = [("xm", blk, n) for n in range(16)]
                    tk.dma("sp", "sp%d" % blk, [(out_d[tb0 + 128 * blk: tb0 + 128 * blk + 128, :], big[:, blk, :])],
                           reads=XK, writes=[("outrows", tau, blk)])
                    token_stats_rstd(lambda c8: big[:, blk, c8 * 512:(c8 + 1) * 512], 128, st8, mv, rsn[:, blk:blk + 1], XK,
                                     ("sn", blk))
                    for c8 in range(8):
                        tk.op("act", lambda e: e.activation(out=big[:, blk, c8 * 512:(c8 + 1) * 512],
                                                            in_=big[:, blk, c8 * 512:(c8 + 1) * 512], func=AF.Identity,
                                                            scale=rsn[:, blk:blk + 1]),
                              reads=[(("sn", blk), "rstd"), ("xm", blk, 2 * c8), ("xm", blk, 2 * c8 + 1)],
                              writes=[("xm", blk, 2 * c8), ("xm", blk, 2 * c8 + 1)])

                def n_tr(blk):
                    XK = [("xm", blk, n) for n in range(16)]
                    to_feature_major(lambda j: big[:, blk, j * 128:(j + 1) * 128], blk, gs_f, sh_f,
                                     ["gs_f"] + VK(2), XK,
                                     lambda j: (actT[:, j, blk * 128:(blk + 1) * 128], ("h2", blk, j)))

                if tau == 0:
                    abufs2 = ada_scope_bufs(sxm)
                    add_slots(sxm, 1)
                    sl_b = ada_load(22)
                n_prep(0)
                for blk in range(4):
                    if blk + 1 < 4:
                        n_prep(blk + 1)
                    n_tr(blk)
                    if tau == 0 and blk == 1:
                        ada_compute(22, sl_b, abufs2)
                        sl_b = ada_load(23)
                if tau == 0:
                    ada_compute(23, sl_b, abufs2)
                XH = [("xmh", n) for n in range(16)]
                token_stats_rstd(lambda c8: xmh[0:1, c8 * 512:(c8 + 1) * 512], 1, st8, mv, mv[0:1, 5:6], XH, "sn")
                tk.op("dve", lambda e: e.tensor_scalar(out=xmh[0:1, :], in0=xmh[0:1, :], scalar1=mv[0:1, 5:6],
                                                       scalar2=None, op0=ALU.mult), reads=XH + [("sn", "rstd")], writes=XH)
                pb = newbank()
                tk.group("pe", [(lambda e, j=j: e.matmul(ps[pb][:, j:j + 1], lhsT=xmh[0:1, j * 128:(j + 1) * 128],
                                                        rhs=one_f[0:1, 0:1], start=True, stop=True)) for j in range(32)],
                         reads=XH + ["one_f"], writes=[("ps", pb)])
                tk.op("dve", lambda e: e.tensor_tensor(out=small[:, 32:64], in0=ps[pb][:, 0:32], in1=gs_f[:], op=ALU.mult),
                      reads=[("ps", pb), "gs_f"], writes=["h2h_t"])
                tk.op("dve", lambda e: e.tensor_tensor(out=actT[:, :, 512], in0=small[:, 32:64], in1=sh_f, op=ALU.add),
                      reads=["h2h_t"] + VK(2), writes=[("h2", 4)])
                tk.barrier()
                if tau == 0:
                    drop_slots(1)
                sxm.close()
                H2M = [("h2", b, j) for b in range(4) for j in range(32)]

                with ExitStack() as sf:
                    act = [sb("act%d" % i, [128, 4, 512], BF16, sf) for i in range(2)]
                    add_slots(sf, 2)
                    g1 = [sb("g1_%d" % i, [128, 512], F32, sf) for i in range(2)]
                    sgroups = [(2 * i, 2 * i + 1) for i in range(21)] + [(42,)]
                    for sgi, gl in enumerate(sgroups):
                        aq = sgi % 2
                        for gi, g in enumerate(gl):
                            sGt = wload(w_k_cols(w_up, g * 256))
                            sVl = wload(w_k_cols(w_up, DFF + g * 256))
                            WG = wK(sGt)
                            WV = wK(sVl)
                            for c2 in range(2):
                                j = 2 * g + c2
                                q = j % 2
                                ai = 2 * gi + c2
                                bG = newbank()
                                tk.group("pe", [(lambda e, k=k: e.matmul(ps[bG][:, 0:512], lhsT=WG[:, k, c2 * 128:(c2 + 1) * 128],
                                                                        rhs=actT[:, k, 0:512], start=(k == 0), stop=(k == 31)))
                                                for k in range(32)], reads=[("W", sGt)] + H2M, writes=[("ps", bG)])
                                hc = q * 8
                                tk.group("pe", [(lambda e, k=k: e.matmul(ps[7][:, hc:hc + 1], lhsT=WG[:, k, c2 * 128:(c2 + 1) * 128],
                                                                        rhs=actT[:, k, 512:513], start=(k == 0), stop=(k == 31)))
                                                for k in range(32)], reads=[("W", sGt), ("h2", 4)], writes=[("ps", 7)])
                                bV = newbank()
                                tk.group("pe", [(lambda e, k=k: e.matmul(ps[bV][:, 0:512], lhsT=WV[:, k, c2 * 128:(c2 + 1) * 128],
                                                                        rhs=actT[:, k, 0:512], start=(k == 0), stop=(k == 31)))
                                                for k in range(32)], reads=[("W", sVl)] + H2M, writes=[("ps", bV)])
                                G1 = ("g1", q)
                                tk.op("act", lambda e: e.activation(out=g1[q][:], in_=ps[bG][:, 0:512], func=AF.Identity,
                                                                    scale=fcw[:, j, 1:2], bias=fcb[:, j:j + 1]),
                                      reads=[("ps", bG), "fcw", "fcb"], writes=[G1])
                                tk.op("dve", lambda e: e.scalar_tensor_tensor(out=g1[q][:, 1:512], in0=ps[bG][:, 0:511],
                                                                              scalar=fcw[:, j, 0:1], in1=g1[q][:, 1:512],
                                                                              op0=ALU.mult, op1=ALU.add),
                                      reads=[("ps", bG), G1], writes=[G1])
                                tk.op("dve", lambda e: e.scalar_tensor_tensor(out=g1[q][:, 0:511], in0=ps[bG][:, 1:512],
                                                                              scalar=fcw[:, j, 2:3], in1=g1[q][:, 0:511],
                                                                              op0=ALU.mult, op1=ALU.add),
                                      reads=[("ps", bG), G1], writes=[G1])
                                tk.op("dve", lambda e: e.scalar_tensor_tensor(out=g1[q][:, 511:512], in0=ps[7][:, hc:hc + 1],
                                                                              scalar=fcw[:, j, 2:3], in1=g1[q][:, 511:512],
                                                                              op0=ALU.mult, op1=ALU.add),
                                      reads=[("ps", 7), G1], writes=[G1])
                                if tau == 0:
                                    tk.op("dve", lambda e: e.tensor_copy(out=gsave[:, j:j + 1], in_=ps[bG][:, 511:512]),
                                          reads=[("ps", bG)], writes=[("gsave", j)])
                                else:
                                    tk.op("dve", lambda e: e.scalar_tensor_tensor(out=g1[q][:, 0:1], in0=gsave[:, j:j + 1],
                                                                                  scalar=fcw[:, j, 0:1], in1=g1[q][:, 0:1],
                                                                                  op0=ALU.mult, op1=ALU.add),
                                          reads=[("gsave", j), G1], writes=[G1])
                                tk.op("act", lambda e: e.activation(out=g1[q][:], in_=g1[q][:], func=AF.Silu), reads=[G1], writes=[G1])
                                tk.op("dve", lambda e: e.tensor_tensor(out=act[aq][:, ai, :], in0=g1[q][:], in1=ps[bV][:, 0:512],
                                                                       op=ALU.mult), reads=[G1, ("ps", bV)], writes=[("act", aq, ai)])
                        sDs = []
                        for g in gl:
                            sDs.append(wload([(lambda t: t[:].rearrange("p (k n) -> p k n", k=2),
                                               w_down[g * 256:(g + 1) * 256, :].rearrange("(k p) n -> p k n", p=128))]))
                        WDs = [Wt[sD][:].rearrange("p (k n) -> p k n", k=2) for sD in sDs]
                        nk = 2 * len(gl)
                        for blk in range(4):
                            for n in range(8):
                                pb = newbank()
                                tk.group("pe", [(lambda e, k=k: e.matmul(ps[pb][:, 0:512], lhsT=act[aq][:, k, blk * 128:(blk + 1) * 128],
                                                                        rhs=WDs[k // 2][:, k % 2, n * 512:(n + 1) * 512],
                                                                        start=(k == 0), stop=(k == nk - 1)))
                                                for k in range(nk)],
                                         reads=[("W", sD) for sD in sDs] + [("act", aq, k) for k in range(nk)], writes=[("ps", pb)])
                                dst = big[:, blk, n * 512:(n + 1) * 512]
                                if sgi == 0:
                                    tk.op("dve", lambda e: e.tensor_copy(out=dst, in_=ps[pb][:, 0:512]), reads=[("ps", pb)],
                                          writes=[("acc", blk, n)])
                                else:
                                    tk.op("dve", lambda e: e.tensor_tensor(out=dst, in0=dst, in1=ps[pb][:, 0:512], op=ALU.add),
                                          reads=[("ps", pb), ("acc", blk, n)], writes=[("acc", blk, n)])
                    tk.barrier()
                    drop_slots(2)
                with ExitStack() as sl:
                    gfb = sb("gfb", [128, D], F32, sl)
                    gfin = sb("gfin", [128, D], F32, sl)
                    xr = [sb("xr%d" % i, [128, D], F32, sl) for i in range(1)]
                    tk.dma("sp", "gfb", [(gfb[:], gsc_d[1, :, :])], reads=[("gsc", 1, r) for r in range(8)], writes=["gfb"])
                    tk.dma("sp", "gfin", [(gfin[:], gfin_d[:, :])], reads=(), writes=["gfin"])
                    for blk in range(4):
                        q = 0
                        AK = [("acc", blk, n) for n in range(8)]
                        tk.dma("sp", "xr%d" % q, [(xr[q][:], out_d[tb0 + 128 * blk: tb0 + 128 * blk + 128, :])],
                               reads=[("outrows", tau, blk)], writes=[("xr", q)])
                        tk.op("dve", lambda e: e.tensor_tensor(out=big[:, blk, :], in0=big[:, blk, :], in1=gfb[:], op=ALU.mult),
                              reads=AK + ["gfb"], writes=AK)
                        tk.op("dve", lambda e: e.tensor_tensor(out=xr[q][:], in0=xr[q][:], in1=big[:, blk, :], op=ALU.add),
                              reads=AK + [("xr", q)], writes=[("xr", q)])
                        token_stats_rstd(lambda c8: xr[q][:, c8 * 512:(c8 + 1) * 512], 128, st8, mv, mv[:, 5:6], [("xr", q)], "sl")
                        tk.op("dve", lambda e: e.scalar_tensor_tensor(out=xr[q][:], in0=xr[q][:], scalar=mv[:, 5:6], in1=gfin[:],
                                                                      op0=ALU.mult, op1=ALU.mult),
                              reads=[("xr", q), ("sl", "rstd"), "gfin"], writes=[("xr", q)])
                        tk.dma("sp", "st%d" % q, [(out_d[tb0 + 128 * blk: tb0 + 128 * blk + 128, :], xr[q][:])],
                               reads=[("xr", q)], writes=[("outrows", tau, blk)])
                    tk.barrier()
        tk.final_wait("sp")
    return nc


def _lay(v, n):
    return np.ascontiguousarray(np.asarray(v, np.float32).reshape(n, 128).T)


def make_in_maps(inputs, cores):
    x = np.asarray(inputs["x"], np.float32)
    c = np.asarray(inputs["c"], np.float32)
    w_ada = np.ascontiguousarray(np.asarray(inputs["w_ada"], np.float32)[0])
    w_in = np.ascontiguousarray(np.asarray(inputs["w_in"], np.float32)[0])
    w_out = np.ascontiguousarray(np.asarray(inputs["w_out"], np.float32)[0])
    w_up = np.ascontiguousarray(np.asarray(inputs["w_up"], np.float32)[0])
    w_down = np.ascontiguousarray(np.asarray(inputs["w_down"], np.float32)[0])
    bada_bc = np.ascontiguousarray(np.broadcast_to(np.asarray(inputs["b_ada"], np.float32)[0][None, :], (128, 6 * D)))
    ident = np.eye(128, dtype=np.float32)
    gfin_bc = np.ascontiguousarray(np.broadcast_to(np.asarray(inputs["g_final"], np.float32)[None, :], (128, D)))
    slng_bc = np.ascontiguousarray(np.broadcast_to(np.asarray(inputs["sgu_ln_g"], np.float32)[0][None, :], (128, DH)))
    slnb_bc = np.ascontiguousarray(np.broadcast_to(np.asarray(inputs["sgu_ln_b"], np.float32)[0][None, :], (128, DH)))
    common = dict(w_ada=w_ada, bada_bc=bada_bc, w_in=w_in, w_out=w_out, w_up=w_up, w_down=w_down, ident=ident,
                  gmix=_lay(inputs["g_mix"][0], 32), gffn=_lay(inputs["g_ffn"][0], 32),
                  convb=_lay(inputs["conv_b"][0], 16), clng=_lay(inputs["conv_ln_g"][0], 16),
                  clnb=_lay(inputs["conv_ln_b"][0], 16), ogc=_lay(inputs["out_g_conv"][0], 16),
                  ogs=_lay(inputs["out_g_sgu"][0], 16), fcb=_lay(inputs["ffn_conv_b"][0], NJ),
                  slng_bc=slng_bc, slnb_bc=slnb_bc, gfin_bc=gfin_bc)
    conv_w = np.asarray(inputs["conv_w"], np.float32)[0]
    fcw = np.asarray(inputs["ffn_conv_w"], np.float32)[0]
    sgu_w = np.asarray(inputs["sgu_w"], np.float32)[0]
    sgu_b = np.asarray(inputs["sgu_b"], np.float32)[0]
    maps = []
    for i in cores:
        b, half = divmod(i, 2)
        rev = half == 1
        xb = x[b]
        cw, fw, sw, sbv = conv_w, fcw, sgu_w, sgu_b
        if rev:
            xb = xb[::-1]
            cw = cw[::-1]
            fw = fw[::-1]
            sw = sw[:, ::-1, ::-1]
            sbv = sbv[:, ::-1]
        m = dict(common)
        m["x_ext"] = np.ascontiguousarray(xb[0:1152])
        m["c_l"] = _lay(c[b], 32)
        m["convw"] = np.ascontiguousarray(cw.T.reshape(16, 128, 31).transpose(1, 0, 2).reshape(128, 16 * 31))
        m["fcw"] = np.ascontiguousarray(fw.T.reshape(NJ, 128, 3).transpose(1, 0, 2).reshape(128, NJ * 3))
        m["wts"] = np.ascontiguousarray(sw.transpose(2, 0, 1).reshape(128, DH))
        m["sb_bc"] = np.ascontiguousarray(np.broadcast_to(sbv.reshape(1, DH), (128, DH)))
        maps.append(m)
    return maps


_NC_CACHE = {}


def kernel(**inputs):
    cores = list(range(8))
    if "nc" not in _NC_CACHE:
        _NC_CACHE["nc"] = build_program()
    nc = _NC_CACHE["nc"]
    in_maps = make_in_maps(inputs, cores)
    res = run_bass_kernel_spmd(nc, in_maps, core_ids=cores)
    out = np.empty((4, 2048, D), np.float32)
    for i in cores:
        b, half = divmod(i, 2)
        o = np.asarray(res.results[i]["out"], np.float32)
        if half == 0:
            out[b, 0:1024] = o
        else:
            out[b, 1024:2048] = o[::-1]
    return out
```

```python
import numpy as np
from contextlib import ExitStack
import concourse.bass as bass
import concourse.mybir as mybir
from concourse.bass_utils import run_bass_kernel_spmd

F32 = mybir.dt.float32
BF16 = mybir.dt.bfloat16
AF = mybir.ActivationFunctionType
ALU = mybir.AluOpType
AX = mybir.AxisListType

D = 4096
DH = 2048
DFF = 11008
NJ = 86
EPS = 1e-6
NSLOT = 3
SAME_SYNC = True
NTILES = 2


class Trk:
    def __init__(self, nc, es):
        self.nc = nc
        self.es = es
        self.eng = {"pe": nc.tensor, "act": nc.scalar, "dve": nc.vector, "sp": nc.sync, "pool": nc.gpsimd}
        self.semh = {}
        self.cnt = {}
        for k in ("pe", "act", "dve"):
            self.semh["s_" + k] = es.enter_context(nc.semaphore("s_" + k))
            self.cnt["s_" + k] = 0
        self.seen = {k: {} for k in self.eng}
        self.lastw = {}
        self.readers = {}
        self.softw = {}
        self._soft_now = set()

    def _collect(self, e, reads, writes):
        own = "s_" + e
        deps = {}

        def add(n, v):
            if deps.get(n, 0) < v:
                deps[n] = v

        for r in reads:
            ev = self.lastw.get(r)
            if ev:
                add(*ev)
        for w in writes:
            ev = self.lastw.get(w)
            if ev and not (ev[0] == own and w in self._soft_now and self.softw.get(w)):
                add(*ev)
            for n, v in self.readers.get(w, {}).items():
                if n == own:
                    continue
                add(n, v)
        if own in deps and (e == "pe" or not SAME_SYNC):
            del deps[own]
        return deps

    def _wait(self, e, deps):
        for n, v in deps.items():
            if self.seen[e].get(n, 0) >= v:
                continue
            self.eng[e].wait_ge(self.semh[n], v)
            self.seen[e][n] = v

    def _record(self, ev, reads, writes):
        n, v = ev
        for r in reads:
            d = self.readers.setdefault(r, {})
            if d.get(n, 0) < v:
                d[n] = v
        for w in writes:
            if w in self._soft_now and self.softw.get(w) and self.lastw.get(w, ("", 0))[0] != n:
                pass
            self.softw[w] = w in self._soft_now
            self.lastw[w] = ev
            self.readers[w] = {}

    def _norm(self, reads, writes):
        isps = lambda k: isinstance(k, tuple) and k[0] == "ps"
        r2 = [k for k in reads if not isps(k)]
        soft = [k for k in reads if isps(k) and k not in writes]
        self._soft_now = set(soft)
        w2 = list(writes) + soft
        return r2, w2

    def op(self, e, fn, reads=(), writes=()):
        reads, writes = self._norm(reads, writes)
        self._wait(e, self._collect(e, reads, writes))
        ins = fn(self.eng[e])
        own = "s_" + e
        self.cnt[own] += 1
        ins.then_inc(self.semh[own], 1)
        self._record((own, self.cnt[own]), reads, writes)

    def group(self, e, fns, reads=(), writes=()):
        reads, writes = self._norm(reads, writes)
        self._wait(e, self._collect(e, reads, writes))
        ins = None
        for fn in fns:
            ins = fn(self.eng[e])
        own = "s_" + e
        self.cnt[own] += 1
        ins.then_inc(self.semh[own], 1)
        self._record((own, self.cnt[own]), reads, writes)

    def dma(self, q, skey, outs_ins, reads=(), writes=()):
        if skey not in self.semh:
            self.semh[skey] = self.es.enter_context(self.nc.semaphore(skey))
            self.cnt[skey] = 0
        self._soft_now = set()
        self._wait(q, self._collect(q, reads, writes))
        for (o, i) in outs_ins:
            self.eng[q].dma_start(out=o, in_=i).then_inc(self.semh[skey], 16)
            self.cnt[skey] += 16
        self._record((skey, self.cnt[skey]), reads, writes)

    def barrier(self, engines=("act", "dve", "sp")):
        for e in engines:
            deps = {}
            for n, c in self.cnt.items():
                if c > 0 and not n.startswith("W"):
                    if n == "s_pe" and e == "pe":
                        continue
                    deps[n] = c
            self._wait(e, deps)

    def final_wait(self, e):
        deps = {n: c for n, c in self.cnt.items() if c > 0}
        self._wait(e, deps)


def build_program():
    nc = bass.Bass("TRN2", target_bir_lowering=False)

    def din(name, shape):
        return nc.dram_tensor(name, list(shape), F32, kind="ExternalInput").ap()

    x_ext = din("x_ext", [1152, D])
    c_l_d = din("c_l", [128, 32])
    w_ada = din("w_ada", [D, 6 * D])
    bada_bc = din("bada_bc", [128, 6 * D])
    w_in = din("w_in", [D, 2 * D])
    w_out = din("w_out", [D, D])
    w_up = din("w_up", [D, 2 * DFF])
    w_down = din("w_down", [DFF, D])
    ident_d = din("ident", [128, 128])
    gmix_d = din("gmix", [128, 32])
    gffn_d = din("gffn", [128, 32])
    convw_d = din("convw", [128, 16 * 31])
    convb_d = din("convb", [128, 16])
    clng_d = din("clng", [128, 16])
    clnb_d = din("clnb", [128, 16])
    ogc_d = din("ogc", [128, 16])
    ogs_d = din("ogs", [128, 16])
    fcw_d = din("fcw", [128, NJ * 3])
    fcb_d = din("fcb", [128, NJ])
    wts_d = din("wts", [128, DH])
    slng_d = din("slng_bc", [128, DH])
    slnb_d = din("slnb_bc", [128, DH])
    sb_d = din("sb_bc", [128, DH])
    gfin_d = din("gfin_bc", [128, D])
    out_d = nc.dram_tensor("out", [1024, D], F32, kind="ExternalOutput").ap()
    gsc_d = nc.dram_tensor("gsc", [2, 128, D], F32, kind="ExternalOutput").ap()

    with ExitStack() as es:
        tk = Trk(nc, es)

        uniq = {"n": 0}

        def sb(name, shape, dt=F32, stack=es):
            uniq["n"] += 1
            return stack.enter_context(nc.sbuf_tensor("sb%d_%s" % (uniq["n"], name), list(shape), dt))

        Wt = [sb("W%d" % i, [128, 8192], BF16) for i in range(NSLOT)]
        ps = [es.enter_context(nc.psum_tensor("ps%d" % i, [128, 512], F32)) for i in range(8)]
        ident = sb("ident", [128, 128])
        ones_bf = sb("ones_bf", [128, 128], BF16)
        one_f = sb("one_f", [128, 2])
        c_l = sb("c_lt", [128, 32])
        c_act = sb("c_act", [128, 32])
        gmix = sb("gmixt", [128, 32])
        gffn = sb("gffnt", [128, 32])
        vecs = sb("vecs", [128, 4, 32])
        gs_m = sb("gs_m", [128, 32])
        gs_f = sb("gs_f", [128, 32])
        convw = sb("convwt", [128, 16, 31])
        convb = sb("convbt", [128, 16])
        clng = sb("clngt", [128, 16])
        clnb = sb("clnbt", [128, 16])
        ogc = sb("ogct", [128, 16])
        ogs = sb("ogst", [128, 16])
        fcw = sb("fcwt", [128, NJ, 3])
        fcb = sb("fcbt", [128, NJ])
        WTs = sb("WTs", [128, 16, 128], BF16)
        asave = sb("asave", [128, 16, 15])
        gsave = sb("gsave", [128, NJ])
        small = sb("small", [128, 64])
        rinv = sb("rinv", [128, 16])
        actT = sb("actT", [128, 32, 513], BF16)

        wstate = {"n": 0, "last": {}}
        nW = {"n": NSLOT}

        pend = set()

        def add_slots(stack, k):
            for _ in range(k):
                Wt.append(sb("Wx%d" % nW["n"], [128, 8192], BF16, stack))
                nW["n"] += 1
                wstate["last"][len(Wt) - 1] = wstate["n"]
                pend.add(len(Wt) - 1)

        def drop_slots(k):
            for _ in range(k):
                Wt.pop()

        def wload(src_list):
            s = min(range(len(Wt)), key=lambda i: wstate["last"].get(i, -1))
            wstate["n"] += 1
            wstate["last"][s] = wstate["n"]
            if s in pend:
                tk.barrier(engines=("pool",))
                pend.clear()
            oi = []
            for (dst, src) in src_list:
                oi.append((dst(Wt[s]), src))
            tk.dma("pool", "Wsem%d" % s, oi, reads=(), writes=[("W", s)])
            return s

        def wK(s):
            return Wt[s][:].rearrange("p (k n) -> p k n", k=32)

        def w_k_cols(W_d, c0):
            return [(lambda t: t[:].rearrange("p (k n) -> p k n", k=32),
                     W_d[:, c0:c0 + 256].rearrange("(k p) n -> p k n", p=128))]

        bstate = {"n": 0}

        def newbank():
            b = bstate["n"] % 6
            bstate["n"] += 1
            return b

        plist = [(ident, ident_d), (c_l, c_l_d), (gmix, gmix_d), (gffn, gffn_d), (convb, convb_d),
                 (clng, clng_d), (clnb, clnb_d), (ogc, ogc_d), (ogs, ogs_d), (fcb, fcb_d)]
        oi = [(t[:], d[:, :]) for t, d in plist]
        oi.append((convw[:].rearrange("p a b -> p (a b)"), convw_d[:, :]))
        oi.append((fcw[:].rearrange("p a b -> p (a b)"), fcw_d[:, :]))
        pkeys = ["ident", "c_l", "gmix", "gffn", "convb", "clng", "clnb", "ogc", "ogs", "fcb", "convw", "fcw"]
        tk.dma("sp", "par", oi, reads=(), writes=pkeys)
        tk.dma("pool", "wts", [(WTs[:].rearrange("p a b -> p (a b)"), wts_d[:, :])], reads=(), writes=["WTs"])
        tk.op("dve", lambda e: e.memset(ones_bf[:], 1.0), writes=["ones_bf"])
        tk.op("dve", lambda e: e.memset(one_f[:], 1.0), writes=["one_f"])

        def rstd_from(var_ap, out_ap, tmp_ap, n, rkeys, wkey, scale=1.0):
            tk.op("dve", lambda e: e.tensor_scalar(out=tmp_ap, in0=var_ap, scalar1=scale, scalar2=EPS,
                                                   op0=ALU.mult, op1=ALU.add), reads=rkeys, writes=[("tmp", wkey)])
            tk.op("act", lambda e: e.activation(out=tmp_ap, in_=tmp_ap, func=AF.Sqrt), reads=[("tmp", wkey)],
                  writes=[("tmp", wkey)])
            tk.op("dve", lambda e: e.reciprocal(out=out_ap, in_=tmp_ap), reads=[("tmp", wkey)], writes=[wkey])

        with ExitStack() as st:
            cT = sb("cT", [128, 32, 128], BF16, st)
            add_slots(st, 5)
            badat = [sb("badat%d" % i, [128, 512], F32, st) for i in range(2)]
            atmp = [sb("atmp%d" % i, [128, 512], F32, st) for i in range(2)]
            gstg = [sb("gstg%d" % i, [128, 512], F32, st) for i in range(2)]
            tk.op("act", lambda e: e.activation(out=c_act[:], in_=c_l[:], func=AF.Silu), reads=["c_l"], writes=["c_act"])
            tk.op("dve", lambda e: e.tensor_copy(out=cT[:], in_=c_act[:].unsqueeze(2).broadcast_to([128, 32, 128])),
                  reads=["c_act"], writes=["cT"])
            for n in range(24):
                which, r = divmod(n, 4)
                slots = [wload([(lambda t: t[:].rearrange("p (k n) -> p k n", k=8),
                                 w_ada[kq * 1024:(kq + 1) * 1024, n * 1024:(n + 1) * 1024].rearrange("(k p) n -> p k n", p=128))])
                         for kq in range(4)]
                W8 = [Wt[sl][:].rearrange("p (k n) -> p k n", k=8) for sl in slots]
                for half in range(2):
                    c0 = n * 1024 + half * 512
                    r2 = 2 * r + half
                    bi = (2 * n + half) % 2
                    tk.dma("sp", "bada%d" % bi, [(badat[bi][:], bada_bc[:, c0:c0 + 512])], reads=(),
                           writes=[("bada", bi)])
                    pb = newbank()
                    tk.group("pe", [(lambda e, k=k: e.matmul(ps[pb][:, 0:512], lhsT=cT[:, k, :],
                                                            rhs=W8[k // 8][:, k % 8, half * 512:(half + 1) * 512],
                                                            start=(k == 0), stop=(k == 31))) for k in range(32)],
                             reads=[("W", sl) for sl in slots] + ["cT"], writes=[("ps", pb)])
                    if which in (2, 5):
                        gi = 0 if which == 2 else 1
                        tk.op("dve", lambda e: e.tensor_tensor(out=gstg[bi][:], in0=ps[pb][:, 0:512], in1=badat[bi][:],
                                                               op=ALU.add),
                              reads=[("ps", pb), ("bada", bi)], writes=[("gstg", bi)])
                        tk.dma("sp", "gst%d" % bi, [(gsc_d[gi, :, r2 * 512:(r2 + 1) * 512], gstg[bi][:])],
                               reads=[("gstg", bi)], writes=[("gsc", gi, r2)])
                    else:
                        vi = {0: 0, 1: 1, 3: 2, 4: 3}[which]
                        tk.op("dve", lambda e: e.tensor_tensor(out=atmp[bi][:], in0=ps[pb][:, 0:512], in1=badat[bi][:],
                                                               op=ALU.add),
                              reads=[("ps", pb), ("bada", bi)], writes=[("atmp", bi)])
                        a3 = atmp[bi][:].rearrange("p (a b) -> p a b", a=4)
                        tk.op("dve", lambda e: e.tensor_tensor(out=a3, in0=a3,
                                                               in1=ident[:].unsqueeze(1).broadcast_to([128, 4, 128]),
                                                               op=ALU.mult),
                              reads=[("atmp", bi), "ident"], writes=[("atmp", bi)])
                        tk.op("dve", lambda e: e.tensor_reduce(out=vecs[:, vi, 4 * r2:4 * r2 + 4], in_=a3, axis=AX.X, op=ALU.add),
                              reads=[("atmp", bi)], writes=[("vecs", vi, r2)])
            for (vi, gsrc, gdst, gk, dk) in ((1, gmix, gs_m, "gmix", "gs_m"), (3, gffn, gs_f, "gffn", "gs_f")):
                tk.op("dve", lambda e: e.tensor_scalar(out=gdst[:], in0=vecs[:, vi, :], scalar1=1.0, scalar2=None,
                                                       op0=ALU.add),
                      reads=[("vecs", vi, r) for r in range(8)], writes=[dk])
                tk.op("dve", lambda e: e.tensor_tensor(out=gdst[:], in0=gdst[:], in1=gsrc[:], op=ALU.mult),
                      reads=[dk, gk], writes=[dk])
            tk.barrier()
            drop_slots(5)
        sh_m = vecs[:, 0, :]
        sh_f = vecs[:, 2, :]
        VK = lambda vi: [("vecs", vi, r) for r in range(8)]

        def token_stats_rstd(src_ap_fn, npart, st8, mv, dst_ap, rkeys, key):
            for c8 in range(8):
                tk.op("dve", lambda e: e.bn_stats(out=st8[0:npart, c8, :], in_=src_ap_fn(c8)), reads=rkeys,
                      writes=[(key, "st", c8)])
            tk.op("dve", lambda e: e.bn_aggr(out=mv[0:npart, 0:2], in_=st8[0:npart].rearrange("p a b -> p (a b)")),
                  reads=[(key, "st", c8) for c8 in range(8)], writes=[(key, "mv")])
            tk.op("dve", lambda e: e.tensor_tensor(out=mv[0:npart, 2:3], in0=mv[0:npart, 0:1], in1=mv[0:npart, 0:1],
                                                   op=ALU.mult), reads=[(key, "mv")], writes=[(key, "m2")])
            tk.op("dve", lambda e: e.tensor_tensor(out=mv[0:npart, 3:4], in0=mv[0:npart, 2:3], in1=mv[0:npart, 1:2],
                                                   op=ALU.add), reads=[(key, "m2"), (key, "mv")], writes=[(key, "ms")])
            rstd_from(mv[0:npart, 3:4], dst_ap, mv[0:npart, 4:5], npart, [(key, "ms")], (key, "rstd"))

        def to_feature_major(src_blk_fn, blk, gs, shv, gkeys, rkeys, dst_key):
            for g in range(8):
                pb = newbank()
                tk.group("pe", [(lambda e, i=i: e.transpose(out=ps[pb][:, i * 128:(i + 1) * 128],
                                                           in_=src_blk_fn((4 * g + i)), identity=ident[:]))
                                for i in range(4)], reads=rkeys + ["ident"], writes=[("ps", pb)])
                for i in range(4):
                    j = 4 * g + i
                    yield_dst = dst_key(j)
                    if g % 2 == 0:
                        tk.op("act", lambda e: e.activation(out=yield_dst[0], in_=ps[pb][:, i * 128:(i + 1) * 128],
                                                            func=AF.Identity, scale=gs[:, j:j + 1], bias=shv[:, j:j + 1]),
                              reads=[("ps", pb)] + gkeys, writes=[yield_dst[1]])
                    else:
                        tk.op("dve", lambda e: e.tensor_scalar(out=yield_dst[0], in0=ps[pb][:, i * 128:(i + 1) * 128],
                                                               scalar1=gs[:, j:j + 1], scalar2=shv[:, j:j + 1],
                                                               op0=ALU.mult, op1=ALU.add),
                              reads=[("ps", pb)] + gkeys, writes=[yield_dst[1]])

        for tau in range(NTILES):
            tb0 = 512 * tau
            yTa = actT[:, 0:16, :]
            yTb = actT[:, 16:32, :]
            with ExitStack() as mx:
                hT = sb("hT", [128, 32, 640], BF16, mx)
                st8 = sb("st8", [128, 8, 6], F32, mx)
                mv = sb("mv", [128, 8], F32, mx)
                with ExitStack() as sx:
                    xt = [sb("xt%d" % i, [128, D], F32, sx) for i in range(2)]
                    def x_prep(blk):
                        xs = blk % 2
                        tk.dma("sp", "xt%d" % xs, [(xt[xs][:], x_ext[tb0 + 128 * blk: tb0 + 128 * blk + 128, :])],
                               reads=(), writes=[("xt", xs)] + [("xtn", xs, c8) for c8 in range(8)])
                        token_stats_rstd(lambda c8: xt[xs][:, c8 * 512:(c8 + 1) * 512], 128, st8, mv, rsx[:, blk:blk + 1],
                                         [("xt", xs)], ("sx", blk))
                        for c8 in range(8):
                            tk.op("act", lambda e: e.activation(out=xt[xs][:, c8 * 512:(c8 + 1) * 512],
                                                                in_=xt[xs][:, c8 * 512:(c8 + 1) * 512], func=AF.Identity,
                                                                scale=rsx[:, blk:blk + 1]),
                                  reads=[("xt", xs), (("sx", blk), "rstd")], writes=[("xtn", xs, c8)])

                    def x_tr(blk):
                        xs = blk % 2
                        to_feature_major(lambda j: xt[xs][:, j * 128:(j + 1) * 128], blk, gs_m, sh_m,
                                         ["gs_m"] + VK(0), [("xt", xs)] + [("xtn", xs, c8) for c8 in range(8)],
                                         lambda j: (hT[:, j, blk * 128:(blk + 1) * 128], ("hT", blk, j)))

                    rsx = sb("rsx", [128, 8], F32, sx)
                    x_prep(0)
                    for blk in range(5):
                        if blk + 1 < 5:
                            x_prep(blk + 1)
                        x_tr(blk)
                    tk.barrier()
                HT = lambda blks: [("hT", b, j) for b in blks for j in range(32)]

                with ExitStack() as sa:
                    conv_out = sb("conv_out", [128, 16, 513], F32, sa)
                    a_pad = [sb("a_pad%d" % i, [128, 543], BF16, sa) for i in range(2)]
                    Dg = [sb("Dg%d" % i, [128, 31, 128], BF16, sa) for i in range(2)]
                    hcv = sb("hcv", [128, 32], F32, sa)
                    sig = [sb("sig%d" % i, [128, 528], F32, sa) for i in range(2)]
                    cbq = [sb("cbq%d" % i, [128, 2, 513], BF16, sa) for i in range(2)]
                    hs = sb("hs", [128, 32], BF16, sa)
                    m1 = sb("m1", [128, 513], F32, sa)
                    m2 = sb("m2", [128, 513], F32, sa)
                    rs = sb("rs", [128, 513], F32, sa)
                    hred = sb("hred", [128, 2], F32, sa)
                    for i in range(2):
                        tk.op("dve", lambda e: e.memset(a_pad[i][:, 0:15], 0.0), writes=[("a_pad", i)])
                    pend_conv = []
                    for ip in range(8):
                        sL = wload(w_k_cols(w_in, ip * 256))
                        sG = wload(w_k_cols(w_in, 2048 + ip * 256))
                        WL = wK(sL)
                        WG = wK(sG)
                        hb = (ip % 2) * 64
                        bL = []
                        for c2 in range(2):
                            pb = newbank()
                            bL.append(pb)
                            tk.group("pe", [(lambda e, k=k: e.matmul(ps[pb][:, 0:512], lhsT=WL[:, k, c2 * 128:(c2 + 1) * 128],
                                                                    rhs=hT[:, k, 0:512], start=(k == 0), stop=(k == 31)))
                                            for k in range(32)],
                                     reads=[("W", sL)] + HT([0, 1, 2, 3]), writes=[("ps", pb)])
                            hc = hb + c2 * 16
                            tk.group("pe", [(lambda e, k=k: e.matmul(ps[7][:, hc:hc + 16], lhsT=WL[:, k, c2 * 128:(c2 + 1) * 128],
                                                                    rhs=hT[:, k, 512:528], start=(k == 0), stop=(k == 31)))
                                            for k in range(32)],
                                     reads=[("W", sL)] + HT([4]), writes=[("ps", 7)])
                        for c2 in range(2):
                            cc = 2 * ip + c2
                            ab = cc % 2
                            pb = newbank()
                            tk.group("pe", [(lambda e, k=k: e.matmul(ps[pb][:, 0:512], lhsT=WG[:, k, c2 * 128:(c2 + 1) * 128],
                                                                    rhs=hT[:, k, 0:512], start=(k == 0), stop=(k == 31)))
                                            for k in range(32)],
                                     reads=[("W", sG)] + HT([0, 1, 2, 3]), writes=[("ps", pb)])
                            hg = hb + 32 + c2 * 16
                            hl = hb + c2 * 16
                            tk.group("pe", [(lambda e, k=k: e.matmul(ps[7][:, hg:hg + 16], lhsT=WG[:, k, c2 * 128:(c2 + 1) * 128],
                                                                    rhs=hT[:, k, 512:528], start=(k == 0), stop=(k == 31)))
                                            for k in range(32)],
                                     reads=[("W", sG)] + HT([4]), writes=[("ps", 7)])
                            for f in pend_conv:
                                f()
                            pend_conv.clear()
                            tk.op("act", lambda e: e.activation(out=sig[ab][:, 0:512], in_=ps[pb][:, 0:512], func=AF.Sigmoid),
                                  reads=[("ps", pb)], writes=[("sig", ab, 0)])
                            tk.op("act", lambda e: e.activation(out=sig[ab][:, 512:528], in_=ps[7][:, hg:hg + 16], func=AF.Sigmoid),
                                  reads=[("ps", 7)], writes=[("sig", ab, 1)])
                            tk.op("dve", lambda e: e.tensor_tensor(out=a_pad[ab][:, 15:527], in0=ps[bL[c2]][:, 0:512],
                                                                   in1=sig[ab][:, 0:512], op=ALU.mult),
                                  reads=[("ps", bL[c2]), ("sig", ab, 0)], writes=[("a_pad", ab)])
                            tk.op("dve", lambda e: e.tensor_tensor(out=a_pad[ab][:, 527:543], in0=ps[7][:, hl:hl + 16],
                                                                   in1=sig[ab][:, 512:528], op=ALU.mult),
                                  reads=[("ps", 7), ("sig", ab, 1)], writes=[("a_pad", ab)])
                            if tau == 0:
                                tk.op("act", lambda e: e.activation(out=asave[:, cc, :], in_=a_pad[ab][:, 512:527], func=AF.Copy),
                                      reads=[("a_pad", ab)], writes=[("asave", cc)])
                            else:
                                tk.op("dve", lambda e: e.tensor_copy(out=a_pad[ab][:, 0:15], in_=asave[:, cc, :]),
                                      reads=[("asave", cc)], writes=[("a_pad", ab)])
                            def conv_emit(cc=cc, ab=ab):
                                tk.op("dve", lambda e: e.tensor_tensor(out=Dg[ab][:], in0=ident[:].unsqueeze(1).broadcast_to([128, 31, 128]),
                                                                       in1=convw[:, cc, :].unsqueeze(2).broadcast_to([128, 31, 128]),
                                                                       op=ALU.mult),
                                      reads=["ident", "convw"], writes=[("Dg", ab)])
                                pc = newbank()
                                tk.group("pe", [(lambda e, k=k: e.matmul(ps[pc][:, 0:512], lhsT=Dg[ab][:, k, :], rhs=a_pad[ab][:, k:k + 512],
                                                                        start=(k == 0), stop=(k == 30))) for k in range(31)],
                                         reads=[("Dg", ab), ("a_pad", ab)], writes=[("ps", pc)])
                                tk.op("act", lambda e: e.activation(out=conv_out[:, cc, 0:512], in_=ps[pc][:, 0:512], func=AF.Identity,
                                                                    bias=convb[:, cc:cc + 1], scale=1.0),
                                      reads=[("ps", pc), "convb"], writes=[("co", cc)])
                                tk.op("dve", lambda e: e.tensor_tensor(out=hcv[:, 0:31], in0=a_pad[ab][:, 512:543], in1=convw[:, cc, :],
                                                                       op=ALU.mult), reads=[("a_pad", ab), "convw"], writes=["hcv"])
                                tk.op("dve", lambda e: e.tensor_reduce(out=hcv[:, 31:32], in_=hcv[:, 0:31], axis=AX.X, op=ALU.add),
                                      reads=["hcv"], writes=["hcv1"])
                                tk.op("dve", lambda e: e.tensor_tensor(out=conv_out[:, cc, 512:513], in0=hcv[:, 31:32],
                                                                       in1=convb[:, cc:cc + 1], op=ALU.add),
                                      reads=["hcv1", "convb"], writes=[("co", cc, "h")])

                            pend_conv.append(conv_emit)
                    for f in pend_conv:
                        f()
                    pend_conv.clear()
                    bS1 = newbank()
                    bS2 = newbank()
                    for cc in range(16):
                        q = cc % 2
                        tk.op("act", lambda e: e.activation(out=cbq[q][:, 0, 0:512], in_=conv_out[:, cc, 0:512], func=AF.Copy),
                              reads=[("co", cc)], writes=[("cbq", q, 0)])
                        tk.op("act", lambda e: e.activation(out=cbq[q][:, 1, 0:512], in_=conv_out[:, cc, 0:512], func=AF.Square),
                              reads=[("co", cc)], writes=[("cbq", q, 1)])
                        tk.group("pe", [lambda e: e.matmul(ps[bS1][:, 0:512], lhsT=ones_bf[:], rhs=cbq[q][:, 0, 0:512],
                                                           start=(cc == 0), stop=(cc == 15))],
                                 reads=[("cbq", q, 0), "ones_bf"], writes=[("ps", bS1)])
                        tk.group("pe", [lambda e: e.matmul(ps[bS2][:, 0:512], lhsT=ones_bf[:], rhs=cbq[q][:, 1, 0:512],
                                                           start=(cc == 0), stop=(cc == 15))],
                                 reads=[("cbq", q, 1), "ones_bf"], writes=[("ps", bS2)])
                    COK = [("co", cc) for cc in range(16)] + [("co", cc, "h") for cc in range(16)]
                    tk.op("act", lambda e: e.activation(out=hs[:, 0:16], in_=conv_out[:, :, 512], func=AF.Copy),
                          reads=COK, writes=["hs0"])
                    tk.op("act", lambda e: e.activation(out=hs[:, 16:32], in_=conv_out[:, :, 512], func=AF.Square),
                          reads=COK, writes=["hs1"])
                    bH = newbank()
                    tk.group("pe", [lambda e: e.matmul(ps[bH][:, 0:32], lhsT=ones_bf[:], rhs=hs[:], start=True, stop=True)],
                             reads=["hs0", "hs1", "ones_bf"], writes=[("ps", bH)])
                    tk.op("dve", lambda e: e.tensor_reduce(out=hred[:], in_=ps[bH][:, 0:32].rearrange("p (a b) -> p a b", a=2),
                                                           axis=AX.X, op=ALU.add), reads=[("ps", bH)], writes=["hred"])
                    tk.op("dve", lambda e: e.tensor_scalar(out=m1[:, 0:512], in0=ps[bS1][:, 0:512], scalar1=1.0 / DH, scalar2=None,
                                                           op0=ALU.mult), reads=[("ps", bS1)], writes=["m1a"])
                    tk.op("dve", lambda e: e.tensor_scalar(out=m2[:, 0:512], in0=ps[bS2][:, 0:512], scalar1=1.0 / DH, scalar2=None,
                                                           op0=ALU.mult), reads=[("ps", bS2)], writes=["m2a"])
                    tk.op("dve", lambda e: e.tensor_scalar(out=m1[:, 512:513], in0=hred[:, 0:1], scalar1=1.0 / DH, scalar2=None,
                                                           op0=ALU.mult), reads=["hred"], writes=["m1b"])
                    tk.op("dve", lambda e: e.tensor_scalar(out=m2[:, 512:513], in0=hred[:, 1:2], scalar1=1.0 / DH, scalar2=None,
                                                           op0=ALU.mult), reads=["hred"], writes=["m2b"])
                    tk.op("dve", lambda e: e.tensor_tensor(out=rs[:], in0=m1[:], in1=m1[:], op=ALU.mult),
                          reads=["m1a", "m1b"], writes=["rs"])
                    tk.op("dve", lambda e: e.tensor_tensor(out=m2[:], in0=m2[:], in1=rs[:], op=ALU.subtract),
                          reads=["m2a", "m2b", "rs"], writes=["m2a", "m2b"])
                    rstd_from(m2[:], rs[:], m2[:], 128, ["m2a", "m2b"], "rsA")
                    for cc in range(16):
                        q = cc % 2
                        acc = conv_out[:, cc, :]
                        tk.op("dve", lambda e: e.tensor_tensor(out=acc, in0=acc, in1=m1[:], op=ALU.subtract),
                              reads=[("co", cc), ("co", cc, "h"), "m1a", "m1b"], writes=[("co", cc), ("co", cc, "h")])
                        tk.op("dve", lambda e: e.tensor_tensor(out=acc, in0=acc, in1=rs[:], op=ALU.mult),
                              reads=[("co", cc), "rsA"], writes=[("co", cc)])
                        tk.op("act", lambda e: e.activation(out=acc, in_=acc, func=AF.Silu, scale=clng[:, cc:cc + 1],
                                                            bias=clnb[:, cc:cc + 1]),
                              reads=[("co", cc), "clng", "clnb"], writes=[("co", cc)])
                        tk.op("dve", lambda e: e.tensor_scalar(out=yTa[:, cc, :], in0=acc, scalar1=ogc[:, cc:cc + 1], scalar2=None,
                                                               op0=ALU.mult), reads=[("co", cc), "ogc"], writes=[("yT", cc)])
                        tk.op("act", lambda e: e.activation(out=cbq[q][:, 0, :], in_=acc, func=AF.Square),
                              reads=[("co", cc)], writes=[("cbq", q, 0)])
                        fns = [(lambda e, b=b: e.matmul(ps[6][:, b * 16 + cc:b * 16 + cc + 1], lhsT=cbq[q][:, 0, b * 128:(b + 1) * 128],
                                                        rhs=ones_bf[:, 0:1], start=True, stop=True)) for b in range(4)]
                        fns.append(lambda e: e.matmul(ps[6][0:1, 64 + cc:65 + cc], lhsT=cbq[q][:, 0, 512:513], rhs=ones_bf[:, 0:1],
                                                      start=True, stop=True))
                        tk.group("pe", fns, reads=[("cbq", q, 0), "ones_bf"], writes=[("ps", 6)])
                    tk.op("dve", lambda e: e.tensor_reduce(out=small[:, 0:4], in_=ps[6][:, 0:64].rearrange("p (a b) -> p a b", a=4),
                                                           axis=AX.X, op=ALU.add), reads=[("ps", 6)], writes=["ssa"])
                    tk.op("dve", lambda e: e.tensor_reduce(out=small[0:1, 4:5], in_=ps[6][0:1, 64:80], axis=AX.X, op=ALU.add),
                          reads=[("ps", 6)], writes=["ssah"])
                    rstd_from(small[:, 0:4], rinv[:, 0:4], small[:, 8:12], 128, ["ssa"], "rinva", scale=1.0 / DH)
                    rstd_from(small[0:1, 4:5], rinv[0:1, 4:5], small[0:1, 12:13], 1, ["ssah"], "rinvah", scale=1.0 / DH)
                    tk.barrier()

                vln = sb("vln", [128, 5, DH], BF16, mx)
                with ExitStack() as sv:
                    zv = sb("zv", [128, 5, DH], F32, sv)
                    slnx = sb("slnx", [128, DH], F32, sv)
                    stv = sb("stv", [128, 5, 8, 6], F32, sv)
                    mvv = sb("mvv", [128, 5, 4], F32, sv)
                    tk.dma("sp", "sln", [(slnx[:], slng_d[:, :])], reads=(), writes=["slnx"])
                    for n in range(8):
                        s = wload(w_k_cols(w_in, 6144 + n * 256))
                        W3 = wK(s)
                        for blk in range(5):
                            pb = newbank()
                            tk.group("pe", [(lambda e, k=k: e.matmul(ps[pb][:, 0:256], lhsT=hT[:, k, blk * 128:(blk + 1) * 128],
                                                                    rhs=W3[:, k, :], start=(k == 0), stop=(k == 31)))
                                            for k in range(32)],
                                     reads=[("W", s)] + HT([blk]), writes=[("ps", pb)])
                            tk.op("act", lambda e: e.activation(out=zv[:, blk, n * 256:(n + 1) * 256], in_=ps[pb][:, 0:256],
                                                                func=AF.Gelu), reads=[("ps", pb)], writes=[("zv", blk, n)])
                            tk.op("dve", lambda e: e.bn_stats(out=stv[:, blk, n, :], in_=zv[:, blk, n * 256:(n + 1) * 256]),
                                  reads=[("zv", blk, n)], writes=[("stv", blk, n)])
                    for blk in range(5):
                        ZK = [("zv", blk, n) for n in range(8)]
                        tk.op("dve", lambda e: e.bn_aggr(out=mvv[:, blk, 0:2], in_=stv[:, blk].rearrange("p a b -> p (a b)")),
                              reads=[("stv", blk, n) for n in range(8)], writes=[("mvv", blk)])
                        rstd_from(mvv[:, blk, 1:2], mvv[:, blk, 2:3], mvv[:, blk, 3:4], 128, [("mvv", blk)], ("mvv", blk, "r"))
                        tk.op("dve", lambda e: e.tensor_scalar(out=zv[:, blk, :], in0=zv[:, blk, :], scalar1=mvv[:, blk, 0:1],
                                                               scalar2=mvv[:, blk, 2:3], op0=ALU.subtract, op1=ALU.mult),
                              reads=ZK + [("mvv", blk), ("mvv", blk, "r")], writes=ZK)
                        tk.op("dve", lambda e: e.tensor_tensor(out=zv[:, blk, :], in0=zv[:, blk, :], in1=slnx[:], op=ALU.mult),
                              reads=ZK + ["slnx"], writes=ZK)
                    tk.dma("sp", "sln", [(slnx[:], slnb_d[:, :])], reads=(), writes=["slnx"])
                    for blk in range(5):
                        ZK = [("zv", blk, n) for n in range(8)]
                        tk.op("dve", lambda e: e.tensor_tensor(out=vln[:, blk, :], in0=zv[:, blk, :], in1=slnx[:], op=ALU.add),
                              reads=ZK + ["slnx"], writes=[("vln", blk)])
                    tk.barrier()

                with ExitStack() as su:
                    sbb = sb("sbb", [128, DH], F32, su)
                    ub = [sb("ub%d" % i, [128, 513], F32, su) for i in range(2)]
                    tb = [sb("tb%d" % i, [128, 513], F32, su) for i in range(2)]
                    sq = [sb("sq%d" % i, [128, 513], BF16, su) for i in range(2)]
                    tk.dma("sp", "sbb", [(sbb[:], sb_d[:, :])], reads=(), writes=["sbb"])
                    pend_stat = []
                    for ip in range(8):
                        s = wload(w_k_cols(w_in, 4096 + ip * 256))
                        W3 = wK(s)
                        for c2 in range(2):
                            h = 2 * ip + c2
                            q = h % 2
                            pb = newbank()
                            tk.group("pe", [(lambda e, k=k: e.matmul(ps[pb][:, 0:512], lhsT=W3[:, k, c2 * 128:(c2 + 1) * 128],
                                                                    rhs=hT[:, k, 0:512], start=(k == 0), stop=(k == 31)))
                                            for k in range(32)],
                                     reads=[("W", s)] + HT([0, 1, 2, 3]), writes=[("ps", pb)])
                            hc = q * 8
                            tk.group("pe", [(lambda e, k=k: e.matmul(ps[7][:, hc:hc + 1], lhsT=W3[:, k, c2 * 128:(c2 + 1) * 128],
                                                                    rhs=hT[:, k, 512:513], start=(k == 0), stop=(k == 31)))
                                            for k in range(32)],
                                     reads=[("W", s)] + HT([4]), writes=[("ps", 7)])
                            for f in pend_stat:
                                f()
                            pend_stat.clear()
                            tk.op("act", lambda e: e.activation(out=ub[q][:, 0:512], in_=ps[pb][:, 0:512], func=AF.Gelu),
                                  reads=[("ps", pb)], writes=[("ub", q, 0)])
                            tk.op("act", lambda e: e.activation(out=ub[q][:, 512:513], in_=ps[7][:, hc:hc + 1], func=AF.Gelu),
                                  reads=[("ps", 7)], writes=[("ub", q, 1)])
                            pm = newbank()
                            fns = [(lambda e, c=c: e.matmul(ps[pm][:, c * 128:(c + 1) * 128], lhsT=vln[:, c, h * 128:(h + 1) * 128],
                                                            rhs=WTs[:, h, :], start=True, stop=True)) for c in range(4)]
                            tk.group("pe", fns, reads=[("vln", c) for c in range(4)] + ["WTs"], writes=[("ps", pm)])
                            tk.group("pe", [lambda e: e.matmul(ps[7][:, hc + 1:hc + 2], lhsT=vln[:, 4, h * 128:(h + 1) * 128],
                                                               rhs=WTs[:, h, 0:1], start=True, stop=True)],
                                     reads=[("vln", 4), "WTs"], writes=[("ps", 7)])
                            tk.op("dve", lambda e: e.tensor_tensor(
                                out=tb[q][:, 0:512].rearrange("p (a b) -> p a b", a=4),
                                in0=ps[pm][:, 0:512].rearrange("p (a b) -> p a b", a=4),
                                in1=sbb[:, h * 128:(h + 1) * 128].unsqueeze(1).broadcast_to([128, 4, 128]), op=ALU.add),
                                  reads=[("ps", pm), "sbb"], writes=[("tb", q, 0)])
                            tk.op("dve", lambda e: e.tensor_tensor(out=tb[q][:, 512:513], in0=ps[7][:, hc + 1:hc + 2],
                                                                   in1=sbb[:, h * 128:h * 128 + 1], op=ALU.add),
                                  reads=[("ps", 7), "sbb"], writes=[("tb", q, 1)])
                            tk.op("dve", lambda e: e.tensor_tensor(out=tb[q][:], in0=tb[q][:], in1=ub[q][:], op=ALU.mult),
                                  reads=[("tb", q, 0), ("tb", q, 1), ("ub", q, 0), ("ub", q, 1)], writes=[("tb", q, 0), ("tb", q, 1)])
                            tk.op("dve", lambda e: e.tensor_scalar(out=yTb[:, h, :], in0=tb[q][:], scalar1=ogs[:, h:h + 1], scalar2=None,
                                                                   op0=ALU.mult), reads=[("tb", q, 0), ("tb", q, 1), "ogs"],
                                  writes=[("yT", 16 + h)])
                            tk.op("act", lambda e: e.activation(out=sq[q][:], in_=tb[q][:], func=AF.Square),
                                  reads=[("tb", q, 0), ("tb", q, 1)], writes=[("sq", q)])
                            def stat_emit(h=h, q=q):
                                fns = [(lambda e, b=b: e.matmul(ps[6][:, b * 16 + h:b * 16 + h + 1], lhsT=sq[q][:, b * 128:(b + 1) * 128],
                                                                rhs=ones_bf[:, 0:1], start=True, stop=True)) for b in range(4)]
                                fns.append(lambda e: e.matmul(ps[6][0:1, 64 + h:65 + h], lhsT=sq[q][:, 512:513], rhs=ones_bf[:, 0:1],
                                                              start=True, stop=True))
                                tk.group("pe", fns, reads=[("sq", q), "ones_bf"], writes=[("ps", 6)])

                            pend_stat.append(stat_emit)
                    for f in pend_stat:
                        f()
                    pend_stat.clear()
                    tk.op("dve", lambda e: e.tensor_reduce(out=small[:, 16:20], in_=ps[6][:, 0:64].rearrange("p (a b) -> p a b", a=4),
                                                           axis=AX.X, op=ALU.add), reads=[("ps", 6)], writes=["ssb"])
                    tk.op("dve", lambda e: e.tensor_reduce(out=small[0:1, 20:21], in_=ps[6][0:1, 64:80], axis=AX.X, op=ALU.add),
                          reads=[("ps", 6)], writes=["ssbh"])
                    rstd_from(small[:, 16:20], rinv[:, 8:12], small[:, 24:28], 128, ["ssb"], "rinvb", scale=1.0 / DH)
                    rstd_from(small[0:1, 20:21], rinv[0:1, 12:13], small[0:1, 28:29], 1, ["ssbh"], "rinvbh", scale=1.0 / DH)
                    tk.barrier()

            with ExitStack() as fx:
                big = sb("big", [128, 4, D], F32, fx)
                st8 = sb("st8b", [128, 8, 6], F32, fx)
                mv = sb("mvb", [128, 8], F32, fx)
                sxm = ExitStack()
                xmh = sb("xmh", [1, D], F32, sxm)
                YT = [("yT", j) for j in range(32)]
                with ExitStack() as so:
                    gmb = sb("gmb", [128, D], F32, so)
                    t1 = [sb("t1_%d" % i, [128, 256], F32, so) for i in range(2)]
                    t2 = [sb("t2_%d" % i, [128, 256], F32, so) for i in range(2)]
                    tk.dma("sp", "gmb", [(gmb[:], gsc_d[0, :, :])], reads=[("gsc", 0, r) for r in range(8)], writes=["gmb"])
                    for blk in range(4):
                        tk.dma("sp", "xm%d" % blk, [(big[:, blk, :], x_ext[tb0 + 128 * blk: tb0 + 128 * blk + 128, :])],
                               reads=(), writes=[("xm", blk, n) for n in range(16)])
                    tk.dma("sp", "xmh", [(xmh[0:1, :], x_ext[tb0 + 512: tb0 + 513, :])], reads=(),
                           writes=[("xmh", n) for n in range(16)])
                    cntr = 0
                    for n in range(16):
                        s = wload(w_k_cols(w_out, n * 256))
                        W3 = wK(s)
                        for blk in range(5):
                            pb = newbank()
                            q = cntr % 2
                            cntr += 1
                            if blk < 4:
                                ts = slice(blk * 128, (blk + 1) * 128)
                                P = slice(0, 128)
                                ra = rinv[:, blk:blk + 1]
                                rb = rinv[:, 8 + blk:9 + blk]
                                dst = big[:, blk, n * 256:(n + 1) * 256]
                                dkey = ("xm", blk, n)
                            else:
                                ts = slice(512, 513)
                                P = slice(0, 1)
                                ra = rinv[0:1, 4:5]
                                rb = rinv[0:1, 12:13]
                                dst = xmh[0:1, n * 256:(n + 1) * 256]
                                dkey = ("xmh", n)
                            fns = [(lambda e, k=k: e.matmul(ps[pb][P, 0:256], lhsT=actT[:, k, ts], rhs=W3[:, k, :],
                                                            start=(k == 0), stop=(k == 15))) for k in range(16)]
                            fns += [(lambda e, k=k: e.matmul(ps[pb][P, 256:512], lhsT=actT[:, k, ts], rhs=W3[:, k, :],
                                                             start=(k == 16), stop=(k == 31))) for k in range(16, 32)]
                            tk.group("pe", fns, reads=[("W", s)] + YT, writes=[("ps", pb)])
                            rk = ["rinva", "rinvah", "rinvb", "rinvbh"]
                            tk.op("act", lambda e: e.activation(out=t1[q][P, :], in_=ps[pb][P, 0:256], func=AF.Identity, scale=ra),
                                  reads=[("ps", pb)] + rk, writes=[("t1", q)])
                            tk.op("dve", lambda e: e.scalar_tensor_tensor(out=t2[q][P, :], in0=ps[pb][P, 256:512], scalar=rb,
                                                                          in1=t1[q][P, :], op0=ALU.mult, op1=ALU.add),
                                  reads=[("ps", pb), ("t1", q)] + rk, writes=[("t2", q)])
                            tk.op("dve", lambda e: e.tensor_tensor(out=t2[q][P, :], in0=t2[q][P, :],
                                                                   in1=gmb[P, n * 256:(n + 1) * 256], op=ALU.mult),
                                  reads=[("t2", q), "gmb"], writes=[("t2", q)])
                            tk.op("dve", lambda e: e.tensor_tensor(out=dst, in0=dst, in1=t2[q][P, :], op=ALU.add),
                                  reads=[("t2", q), dkey], writes=[dkey])
                    tk.barrier()
                rsn = sb("rsn", [128, 8], F32, sxm)

                def n_prep(blk):
                    XK = [("xm", blk, n) for n in range(16)]
                    tk.dma("sp", "sp%d" % blk, [(out_d[tb0 + 128 * blk: tb0 + 128 * blk + 128, :], big[:, blk, :])],
                           reads=XK, writes=[("outrows", tau, blk)])
                    token_stats_rstd(lambda c8: big[:, blk, c8 * 512:(c8 + 1) * 512], 128, st8, mv, rsn[:, blk:blk + 1], XK,
                                     ("sn", blk))
                    for c8 in range(8):
                        tk.op("act", lambda e: e.activation(out=big[:, blk, c8 * 512:(c8 + 1) * 512],
                                                            in_=big[:, blk, c8 * 512:(c8 + 1) * 512], func=AF.Identity,
                                                            scale=rsn[:, blk:blk + 1]),
                              reads=[(("sn", blk), "rstd"), ("xm", blk, 2 * c8), ("xm", blk, 2 * c8 + 1)],
                              writes=[("xm", blk, 2 * c8), ("xm", blk, 2 * c8 + 1)])

                def n_tr(blk):
                    XK = [("xm", blk, n) for n in range(16)]
                    to_feature_major(lambda j: big[:, blk, j * 128:(j + 1) * 128], blk, gs_f, sh_f,
                                     ["gs_f"] + VK(2), XK,
                                     lambda j: (actT[:, j, blk * 128:(blk + 1) * 128], ("h2", blk, j)))

                n_prep(0)
                for blk in range(4):
                    if blk + 1 < 4:
                        n_prep(blk + 1)
                    n_tr(blk)
                XH = [("xmh", n) for n in range(16)]
                token_stats_rstd(lambda c8: xmh[0:1, c8 * 512:(c8 + 1) * 512], 1, st8, mv, mv[0:1, 5:6], XH, "sn")
                tk.op("dve", lambda e: e.tensor_scalar(out=xmh[0:1, :], in0=xmh[0:1, :], scalar1=mv[0:1, 5:6],
                                                       scalar2=None, op0=ALU.mult), reads=XH + [("sn", "rstd")], writes=XH)
                pb = newbank()
                tk.group("pe", [(lambda e, j=j: e.matmul(ps[pb][:, j:j + 1], lhsT=xmh[0:1, j * 128:(j + 1) * 128],
                                                        rhs=one_f[0:1, 0:1], start=True, stop=True)) for j in range(32)],
                         reads=XH + ["one_f"], writes=[("ps", pb)])
                tk.op("dve", lambda e: e.tensor_tensor(out=small[:, 32:64], in0=ps[pb][:, 0:32], in1=gs_f[:], op=ALU.mult),
                      reads=[("ps", pb), "gs_f"], writes=["h2h_t"])
                tk.op("dve", lambda e: e.tensor_tensor(out=actT[:, :, 512], in0=small[:, 32:64], in1=sh_f, op=ALU.add),
                      reads=["h2h_t"] + VK(2), writes=[("h2", 4)])
                tk.barrier()
                sxm.close()
                H2M = [("h2", b, j) for b in range(4) for j in range(32)]

                with ExitStack() as sf:
                    act = [sb("act%d" % i, [128, 4, 512], BF16, sf) for i in range(2)]
                    add_slots(sf, 2)
                    g1 = [sb("g1_%d" % i, [128, 512], F32, sf) for i in range(2)]
                    sgroups = [(2 * i, 2 * i + 1) for i in range(21)] + [(42,)]
                    for sgi, gl in enumerate(sgroups):
                        aq = sgi % 2
                        for gi, g in enumerate(gl):
                            sGt = wload(w_k_cols(w_up, g * 256))
                            sVl = wload(w_k_cols(w_up, DFF + g * 256))
                            WG = wK(sGt)
                            WV = wK(sVl)
                            for c2 in range(2):
                                j = 2 * g + c2
                                q = j % 2
                                ai = 2 * gi + c2
                                bG = newbank()
                                tk.group("pe", [(lambda e, k=k: e.matmul(ps[bG][:, 0:512], lhsT=WG[:, k, c2 * 128:(c2 + 1) * 128],
                                                                        rhs=actT[:, k, 0:512], start=(k == 0), stop=(k == 31)))
                                                for k in range(32)], reads=[("W", sGt)] + H2M, writes=[("ps", bG)])
                                hc = q * 8
                                tk.group("pe", [(lambda e, k=k: e.matmul(ps[7][:, hc:hc + 1], lhsT=WG[:, k, c2 * 128:(c2 + 1) * 128],
                                                                        rhs=actT[:, k, 512:513], start=(k == 0), stop=(k == 31)))
                                                for k in range(32)], reads=[("W", sGt), ("h2", 4)], writes=[("ps", 7)])
                                bV = newbank()
                                tk.group("pe", [(lambda e, k=k: e.matmul(ps[bV][:, 0:512], lhsT=WV[:, k, c2 * 128:(c2 + 1) * 128],
                                                                        rhs=actT[:, k, 0:512], start=(k == 0), stop=(k == 31)))
                                                for k in range(32)], reads=[("W", sVl)] + H2M, writes=[("ps", bV)])
                                G1 = ("g1", q)
                                tk.op("act", lambda e: e.activation(out=g1[q][:], in_=ps[bG][:, 0:512], func=AF.Identity,
                                                                    scale=fcw[:, j, 1:2], bias=fcb[:, j:j + 1]),
                                      reads=[("ps", bG), "fcw", "fcb"], writes=[G1])
                                tk.op("dve", lambda e: e.scalar_tensor_tensor(out=g1[q][:, 1:512], in0=ps[bG][:, 0:511],
                                                                              scalar=fcw[:, j, 0:1], in1=g1[q][:, 1:512],
                                                                              op0=ALU.mult, op1=ALU.add),
                                      reads=[("ps", bG), G1], writes=[G1])
                                tk.op("dve", lambda e: e.scalar_tensor_tensor(out=g1[q][:, 0:511], in0=ps[bG][:, 1:512],
                                                                              scalar=fcw[:, j, 2:3], in1=g1[q][:, 0:511],
                                                                              op0=ALU.mult, op1=ALU.add),
                                      reads=[("ps", bG), G1], writes=[G1])
                                tk.op("dve", lambda e: e.scalar_tensor_tensor(out=g1[q][:, 511:512], in0=ps[7][:, hc:hc + 1],
                                                                              scalar=fcw[:, j, 2:3], in1=g1[q][:, 511:512],
                                                                              op0=ALU.mult, op1=ALU.add),
                                      reads=[("ps", 7), G1], writes=[G1])
                                if tau == 0:
                                    tk.op("dve", lambda e: e.tensor_copy(out=gsave[:, j:j + 1], in_=ps[bG][:, 511:512]),
                                          reads=[("ps", bG)], writes=[("gsave", j)])
                                else:
                                    tk.op("dve", lambda e: e.scalar_tensor_tensor(out=g1[q][:, 0:1], in0=gsave[:, j:j + 1],
                                                                                  scalar=fcw[:, j, 0:1], in1=g1[q][:, 0:1],
                                                                                  op0=ALU.mult, op1=ALU.add),
                                          reads=[("gsave", j), G1], writes=[G1])
                                tk.op("act", lambda e: e.activation(out=g1[q][:], in_=g1[q][:], func=AF.Silu), reads=[G1], writes=[G1])
                                tk.op("dve", lambda e: e.tensor_tensor(out=act[aq][:, ai, :], in0=g1[q][:], in1=ps[bV][:, 0:512],
                                                                       op=ALU.mult), reads=[G1, ("ps", bV)], writes=[("act", aq, ai)])
                        sDs = []
                        for g in gl:
                            sDs.append(wload([(lambda t: t[:].rearrange("p (k n) -> p k n", k=2),
                                               w_down[g * 256:(g + 1) * 256, :].rearrange("(k p) n -> p k n", p=128))]))
                        WDs = [Wt[sD][:].rearrange("p (k n) -> p k n", k=2) for sD in sDs]
                        nk = 2 * len(gl)
                        for blk in range(4):
                            for n in range(8):
                                pb = newbank()
                                tk.group("pe", [(lambda e, k=k: e.matmul(ps[pb][:, 0:512], lhsT=act[aq][:, k, blk * 128:(blk + 1) * 128],
                                                                        rhs=WDs[k // 2][:, k % 2, n * 512:(n + 1) * 512],
                                                                        start=(k == 0), stop=(k == nk - 1)))
                                                for k in range(nk)],
                                         reads=[("W", sD) for sD in sDs] + [("act", aq, k) for k in range(nk)], writes=[("ps", pb)])
                                dst = big[:, blk, n * 512:(n + 1) * 512]
                                if sgi == 0:
                                    tk.op("dve", lambda e: e.tensor_copy(out=dst, in_=ps[pb][:, 0:512]), reads=[("ps", pb)],
                                          writes=[("acc", blk, n)])
                                else:
                                    tk.op("dve", lambda e: e.tensor_tensor(out=dst, in0=dst, in1=ps[pb][:, 0:512], op=ALU.add),
                                          reads=[("ps", pb), ("acc", blk, n)], writes=[("acc", blk, n)])
                    tk.barrier()
                    drop_slots(2)
                with ExitStack() as sl:
                    gfb = sb("gfb", [128, D], F32, sl)
                    gfin = sb("gfin", [128, D], F32, sl)
                    xr = [sb("xr%d" % i, [128, D], F32, sl) for i in range(1)]
                    tk.dma("sp", "gfb", [(gfb[:], gsc_d[1, :, :])], reads=[("gsc", 1, r) for r in range(8)], writes=["gfb"])
                    tk.dma("sp", "gfin", [(gfin[:], gfin_d[:, :])], reads=(), writes=["gfin"])
                    for blk in range(4):
                        q = 0
                        AK = [("acc", blk, n) for n in range(8)]
                        tk.dma("sp", "xr%d" % q, [(xr[q][:], out_d[tb0 + 128 * blk: tb0 + 128 * blk + 128, :])],
                               reads=[("outrows", tau, blk)], writes=[("xr", q)])
                        tk.op("dve", lambda e: e.tensor_tensor(out=big[:, blk, :], in0=big[:, blk, :], in1=gfb[:], op=ALU.mult),
                              reads=AK + ["gfb"], writes=AK)
                        tk.op("dve", lambda e: e.tensor_tensor(out=xr[q][:], in0=xr[q][:], in1=big[:, blk, :], op=ALU.add),
                              reads=AK + [("xr", q)], writes=[("xr", q)])
                        token_stats_rstd(lambda c8: xr[q][:, c8 * 512:(c8 + 1) * 512], 128, st8, mv, mv[:, 5:6], [("xr", q)], "sl")
                        tk.op("dve", lambda e: e.scalar_tensor_tensor(out=xr[q][:], in0=xr[q][:], scalar=mv[:, 5:6], in1=gfin[:],
                                                                      op0=ALU.mult, op1=ALU.mult),
                              reads=[("xr", q), ("sl", "rstd"), "gfin"], writes=[("xr", q)])
                        tk.dma("sp", "st%d" % q, [(out_d[tb0 + 128 * blk: tb0 + 128 * blk + 128, :], xr[q][:])],
                               reads=[("xr", q)], writes=[("outrows", tau, blk)])
                    tk.barrier()
        tk.final_wait("sp")
    return nc


def _lay(v, n):
    return np.ascontiguousarray(np.asarray(v, np.float32).reshape(n, 128).T)


def make_in_maps(inputs, cores):
    x = np.asarray(inputs["x"], np.float32)
    c = np.asarray(inputs["c"], np.float32)
    w_ada = np.ascontiguousarray(np.asarray(inputs["w_ada"], np.float32)[0])
    w_in = np.ascontiguousarray(np.asarray(inputs["w_in"], np.float32)[0])
    w_out = np.ascontiguousarray(np.asarray(inputs["w_out"], np.float32)[0])
    w_up = np.ascontiguousarray(np.asarray(inputs["w_up"], np.float32)[0])
    w_down = np.ascontiguousarray(np.asarray(inputs["w_down"], np.float32)[0])
    bada_bc = np.ascontiguousarray(np.broadcast_to(np.asarray(inputs["b_ada"], np.float32)[0][None, :], (128, 6 * D)))
    ident = np.eye(128, dtype=np.float32)
    gfin_bc = np.ascontiguousarray(np.broadcast_to(np.asarray(inputs["g_final"], np.float32)[None, :], (128, D)))
    slng_bc = np.ascontiguousarray(np.broadcast_to(np.asarray(inputs["sgu_ln_g"], np.float32)[0][None, :], (128, DH)))
    slnb_bc = np.ascontiguousarray(np.broadcast_to(np.asarray(inputs["sgu_ln_b"], np.float32)[0][None, :], (128, DH)))
    common = dict(w_ada=w_ada, bada_bc=bada_bc, w_in=w_in, w_out=w_out, w_up=w_up, w_down=w_down, ident=ident,
                  gmix=_lay(inputs["g_mix"][0], 32), gffn=_lay(inputs["g_ffn"][0], 32),
                  convb=_lay(inputs["conv_b"][0], 16), clng=_lay(inputs["conv_ln_g"][0], 16),
                  clnb=_lay(inputs["conv_ln_b"][0], 16), ogc=_lay(inputs["out_g_conv"][0], 16),
                  ogs=_lay(inputs["out_g_sgu"][0], 16), fcb=_lay(inputs["ffn_conv_b"][0], NJ),
                  slng_bc=slng_bc, slnb_bc=slnb_bc, gfin_bc=gfin_bc)
    conv_w = np.asarray(inputs["conv_w"], np.float32)[0]
    fcw = np.asarray(inputs["ffn_conv_w"], np.float32)[0]
    sgu_w = np.asarray(inputs["sgu_w"], np.float32)[0]
    sgu_b = np.asarray(inputs["sgu_b"], np.float32)[0]
    maps = []
    for i in cores:
        b, half = divmod(i, 2)
        rev = half == 1
        xb = x[b]
        cw, fw, sw, sbv = conv_w, fcw, sgu_w, sgu_b
        if rev:
            xb = xb[::-1]
            cw = cw[::-1]
            fw = fw[::-1]
            sw = sw[:, ::-1, ::-1]
            sbv = sbv[:, ::-1]
        m = dict(common)
        m["x_ext"] = np.ascontiguousarray(xb[0:1152])
        m["c_l"] = _lay(c[b], 32)
        m["convw"] = np.ascontiguousarray(cw.T.reshape(16, 128, 31).transpose(1, 0, 2).reshape(128, 16 * 31))
        m["fcw"] = np.ascontiguousarray(fw.T.reshape(NJ, 128, 3).transpose(1, 0, 2).reshape(128, NJ * 3))
        m["wts"] = np.ascontiguousarray(sw.transpose(2, 0, 1).reshape(128, DH))
        m["sb_bc"] = np.ascontiguousarray(np.broadcast_to(sbv.reshape(1, DH), (128, DH)))
        maps.append(m)
    return maps


_NC_CACHE = {}


def kernel(**inputs):
    cores = list(range(8))
    if "nc" not in _NC_CACHE:
        _NC_CACHE["nc"] = build_program()
    nc = _NC_CACHE["nc"]
    in_maps = make_in_maps(inputs, cores)
    res = run_bass_kernel_spmd(nc, in_maps, core_ids=cores)
    out = np.empty((4, 2048, D), np.float32)
    for i in cores:
        b, half = divmod(i, 2)
        o = np.asarray(res.results[i]["out"], np.float32)
        if half == 0:
            out[b, 0:1024] = o
        else:
            out[b, 1024:2048] = o[::-1]
    return out
```
